# Optimizing a Trainium2 kernel written in Bass

```python
import math
import jax, jax.numpy as jnp
from jax import lax
import numpy as np

D_MODEL = 2048
BATCH = 2
SEQ = 8192
DEPTH = 2

D_MIX = D_MODEL
SSM_WIDTH = D_MIX // 2
SSM_GROUP = 16
SSM_GROUPS = SSM_WIDTH // SSM_GROUP
SSM_STATE = 64
SSM_CHUNK = 128
DIFF_WIDTH = D_MIX // 4
DIFF_HEADS = 4
DIFF_HEAD_DIM = DIFF_WIDTH // (2 * DIFF_HEADS)
DIFF_V_DIM = 2 * DIFF_HEAD_DIM
XATTN_WIDTH = D_MIX - SSM_WIDTH - DIFF_WIDTH
XATTN_HEADS = 4
XATTN_HEAD_DIM = XATTN_WIDTH // XATTN_HEADS
MEM_TOKENS = 256
ROPE_THETA = 10000.0
Q_BLOCK = 128
NORM_EPS = 1e-6
MASK_VALUE = -1e30

kernel_name = "hybrid_s5_diffattn_memxattn_block"


def rmsnorm(x, g):
    x32 = x.astype(jnp.float32)
    var = jnp.mean(x32 * x32, axis=-1, keepdims=True)
    return (x32 * lax.rsqrt(var + NORM_EPS) * g.astype(jnp.float32)).astype(x.dtype)


def rope_tables(positions, dim):
    inv = ROPE_THETA ** (-jnp.arange(0, dim, 2, dtype=jnp.float32) / dim)
    ang = positions.astype(jnp.float32)[..., None] * inv
    return jnp.cos(ang), jnp.sin(ang)


def apply_rope(x, cos, sin):
    extra = x.ndim - 3
    shp = cos.shape[:2] + (1,) * extra + cos.shape[-1:]
    c, s = cos.reshape(shp), sin.reshape(shp)
    x1, x2 = jnp.split(x.astype(jnp.float32), 2, axis=-1)
    return jnp.concatenate([x1 * c - x2 * s, x2 * c + x1 * s], axis=-1).astype(x.dtype)


def s5_scan(u, a_re, a_im, log_dt, b_re, b_im, c_re, c_im, d_skip):
    f32 = jnp.float32
    bsz, seq = u.shape[0], u.shape[1]
    n_chunks = seq // SSM_CHUNK
    dt = jnp.exp(log_dt.astype(f32))[:, None]
    lr, li = a_re.astype(f32), a_im.astype(f32)
    mag = jnp.exp(lr * dt)
    abar_re, abar_im = mag * jnp.cos(li * dt), mag * jnp.sin(li * dt)
    den = lr * lr + li * li
    nr, ni = abar_re - 1.0, abar_im
    z_re = (nr * lr + ni * li) / den
    z_im = (ni * lr - nr * li) / den
    br, bi = b_re.astype(f32), b_im.astype(f32)
    bbar_re = z_re[..., None] * br - z_im[..., None] * bi
    bbar_im = z_re[..., None] * bi + z_im[..., None] * br
    cr, ci = c_re.astype(f32), c_im.astype(f32)

    uc = u.astype(f32).reshape(bsz, n_chunks, SSM_CHUNK, SSM_GROUPS, SSM_GROUP)
    uc = uc.transpose(1, 2, 0, 3, 4)
    a_el_re = jnp.broadcast_to(abar_re[None, None], (SSM_CHUNK, 1, SSM_GROUPS, SSM_STATE))
    a_el_im = jnp.broadcast_to(abar_im[None, None], (SSM_CHUNK, 1, SSM_GROUPS, SSM_STATE))

    def combine(e1, e2):
        a1r, a1i, b1r, b1i = e1
        a2r, a2i, b2r, b2i = e2
        return (a2r * a1r - a2i * a1i,
                a2r * a1i + a2i * a1r,
                a2r * b1r - a2i * b1i + b2r,
                a2r * b1i + a2i * b1r + b2i)

    def chunk_step(carry, u_t):
        h_re, h_im = carry
        bu_re = jnp.einsum('tbgh,gph->tbgp', u_t, bbar_re)
        bu_im = jnp.einsum('tbgh,gph->tbgp', u_t, bbar_im)
        pa_re, pa_im, s_re, s_im = lax.associative_scan(
            combine, (a_el_re, a_el_im, bu_re, bu_im), axis=0)
        x_re = s_re + pa_re * h_re - pa_im * h_im
        x_im = s_im + pa_re * h_im + pa_im * h_re
        y = jnp.einsum('tbgp,ghp->tbgh', x_re, cr) - jnp.einsum('tbgp,ghp->tbgh', x_im, ci)
        return (x_re[-1], x_im[-1]), y

    h0 = (jnp.zeros((bsz, SSM_GROUPS, SSM_STATE), f32), jnp.zeros((bsz, SSM_GROUPS, SSM_STATE), f32))
    _, ys = lax.scan(chunk_step, h0, uc)
    y = ys.transpose(2, 0, 1, 3, 4).reshape(bsz, seq, SSM_WIDTH)
    y = y + d_skip.astype(f32).reshape(SSM_WIDTH) * u.astype(f32)
    return y.astype(u.dtype)


def diff_attention(q, k, v, lam, lambda_init, subln_g):
    bsz, seq = q.shape[0], q.shape[1]
    n_blocks = seq // Q_BLOCK
    scale = DIFF_HEAD_DIM ** -0.5
    k_pos = jnp.arange(seq)
    qb = q.reshape(bsz, n_blocks, Q_BLOCK, DIFF_HEADS, 2, DIFF_HEAD_DIM).transpose(1, 0, 2, 3, 4, 5)

    def block(args):
        qi, blk = args
        s = jnp.einsum('bqhcd,bkhcd->bhcqk', qi, k).astype(jnp.float32) * scale
        q_pos = blk * Q_BLOCK + jnp.arange(Q_BLOCK)
        mask = k_pos[None, :] <= q_pos[:, None]
        s = jnp.where(mask, s, MASK_VALUE)
        p = jax.nn.softmax(s, axis=-1)
        w = p[:, :, 0] - lam * p[:, :, 1]
        return jnp.einsum('bhqk,bkhd->bqhd', w.astype(v.dtype), v)

    o = lax.map(block, (qb, jnp.arange(n_blocks)))
    o = o.transpose(1, 0, 2, 3, 4).reshape(bsz, seq, DIFF_HEADS, DIFF_V_DIM)
    o = rmsnorm(o, subln_g) * (1.0 - lambda_init)
    return o.reshape(bsz, seq, DIFF_WIDTH)


def memory_attention(q, mem_n, w_mem_kv):
    bsz, seq = q.shape[0], q.shape[1]
    kv = mem_n @ w_mem_kv
    mk, mv = jnp.split(kv, 2, axis=-1)
    mk = mk.reshape(bsz, MEM_TOKENS, XATTN_HEADS, XATTN_HEAD_DIM)
    mv = mv.reshape(bsz, MEM_TOKENS, XATTN_HEADS, XATTN_HEAD_DIM)
    s = jnp.einsum('bqhd,bmhd->bhqm', q, mk).astype(jnp.float32) * (XATTN_HEAD_DIM ** -0.5)
    p = jax.nn.softmax(s, axis=-1).astype(mv.dtype)
    o = jnp.einsum('bhqm,bmhd->bqhd', p, mv)
    return o.reshape(bsz, seq, XATTN_WIDTH)


def setup_inputs(seed: int = 0) -> dict:
    key = jax.random.key(seed)
    ks = jax.random.split(key, 24)
    f32 = jnp.float32
    in_cols = 2 * SSM_WIDTH + 4 * DIFF_WIDTH + 2 * XATTN_WIDTH
    nrm = lambda k, shp, s: jax.random.normal(k, shp, f32) * s
    x = jax.random.normal(ks[0], (BATCH, SEQ, D_MODEL), f32)
    mem = jax.random.normal(ks[1], (BATCH, MEM_TOKENS, D_MODEL), f32)
    positions = jnp.broadcast_to(jnp.arange(SEQ, dtype=jnp.int32), (BATCH, SEQ))
    norm_pre = 1.0 + nrm(ks[2], (DEPTH, D_MODEL), 0.02)
    norm_post = 1.0 + nrm(ks[3], (DEPTH, D_MODEL), 0.02)
    norm_mem = 1.0 + nrm(ks[4], (DEPTH, D_MODEL), 0.02)
    w_in = nrm(ks[5], (DEPTH, D_MODEL, in_cols), D_MODEL ** -0.5)
    w_out = nrm(ks[6], (DEPTH, D_MIX, D_MODEL), D_MIX ** -0.5)
    w_mem_kv = nrm(ks[7], (DEPTH, D_MODEL, 2 * XATTN_WIDTH), D_MODEL ** -0.5)
    gp = (DEPTH, SSM_GROUPS, SSM_STATE)
    ssm_a_re = -0.5 + nrm(ks[8], gp, 0.01)
    ssm_a_im = jnp.pi * jnp.arange(SSM_STATE, dtype=f32)[None, None, :] + nrm(ks[9], gp, 0.01)
    ssm_log_dt = jax.random.uniform(ks[10], (DEPTH, SSM_GROUPS), f32, math.log(1e-3), math.log(1e-1))
    ssm_b_re = nrm(ks[11], (DEPTH, SSM_GROUPS, SSM_STATE, SSM_GROUP), (2 * SSM_GROUP) ** -0.5)
    ssm_b_im = nrm(ks[12], (DEPTH, SSM_GROUPS, SSM_STATE, SSM_GROUP), (2 * SSM_GROUP) ** -0.5)
    ssm_c_re = nrm(ks[13], (DEPTH, SSM_GROUPS, SSM_GROUP, SSM_STATE), (2 * SSM_STATE) ** -0.5)
    ssm_c_im = nrm(ks[14], (DEPTH, SSM_GROUPS, SSM_GROUP, SSM_STATE), (2 * SSM_STATE) ** -0.5)
    ssm_d = nrm(ks[15], (DEPTH, SSM_GROUPS, SSM_GROUP), 1.0)
    w_glu = nrm(ks[16], (DEPTH, SSM_WIDTH, SSM_WIDTH), SSM_WIDTH ** -0.5)
    b_glu = nrm(ks[17], (DEPTH, SSM_WIDTH), 0.01)
    diff_lq1 = nrm(ks[18], (DEPTH, DIFF_HEAD_DIM), 0.1)
    diff_lk1 = nrm(ks[19], (DEPTH, DIFF_HEAD_DIM), 0.1)
    diff_lq2 = nrm(ks[20], (DEPTH, DIFF_HEAD_DIM), 0.1)
    diff_lk2 = nrm(ks[21], (DEPTH, DIFF_HEAD_DIM), 0.1)
    diff_subln = 1.0 + nrm(ks[22], (DEPTH, DIFF_V_DIM), 0.02)
    return {"x": x, "mem": mem, "positions": positions,
            "norm_pre": norm_pre, "norm_post": norm_post, "norm_mem": norm_mem,
            "w_in": w_in, "w_out": w_out, "w_mem_kv": w_mem_kv,
            "ssm_a_re": ssm_a_re, "ssm_a_im": ssm_a_im, "ssm_log_dt": ssm_log_dt,
            "ssm_b_re": ssm_b_re, "ssm_b_im": ssm_b_im, "ssm_c_re": ssm_c_re, "ssm_c_im": ssm_c_im,
            "ssm_d": ssm_d, "w_glu": w_glu, "b_glu": b_glu,
            "diff_lq1": diff_lq1, "diff_lk1": diff_lk1, "diff_lq2": diff_lq2, "diff_lk2": diff_lk2,
            "diff_subln": diff_subln}


def reference(x, mem, positions, norm_pre, norm_post, norm_mem, w_in, w_out, w_mem_kv,
              ssm_a_re, ssm_a_im, ssm_log_dt, ssm_b_re, ssm_b_im, ssm_c_re, ssm_c_im,
              ssm_d, w_glu, b_glu, diff_lq1, diff_lk1, diff_lq2, diff_lk2, diff_subln):
    bsz, seq = x.shape[0], x.shape[1]
    cos, sin = rope_tables(positions, DIFF_HEAD_DIM)
    widths = [SSM_WIDTH, SSM_WIDTH, DIFF_WIDTH, DIFF_WIDTH, DIFF_WIDTH, DIFF_WIDTH,
              XATTN_WIDTH, XATTN_WIDTH]
    split_at = [sum(widths[:i + 1]) for i in range(len(widths) - 1)]
    for l in range(DEPTH):
        lambda_init = 0.8 - 0.6 * math.exp(-0.3 * l)
        h = rmsnorm(x, norm_pre[l])
        proj = h @ w_in[l]
        u_s, g_s, q_d, k_d, v_d, g_d, q_x, g_x = jnp.split(proj, split_at, axis=-1)

        y_s = s5_scan(u_s, ssm_a_re[l], ssm_a_im[l], ssm_log_dt[l], ssm_b_re[l], ssm_b_im[l],
                      ssm_c_re[l], ssm_c_im[l], ssm_d[l])
        y_s = jax.nn.gelu(y_s)
        y_s = y_s * jax.nn.sigmoid(y_s @ w_glu[l] + b_glu[l])
        y_s = y_s * jax.nn.silu(g_s)

        q = apply_rope(q_d.reshape(bsz, seq, DIFF_HEADS, 2, DIFF_HEAD_DIM), cos, sin)
        k = apply_rope(k_d.reshape(bsz, seq, DIFF_HEADS, 2, DIFF_HEAD_DIM), cos, sin)
        v = v_d.reshape(bsz, seq, DIFF_HEADS, DIFF_V_DIM)
        lam = (jnp.exp(jnp.sum(diff_lq1[l].astype(jnp.float32) * diff_lk1[l].astype(jnp.float32)))
               - jnp.exp(jnp.sum(diff_lq2[l].astype(jnp.float32) * diff_lk2[l].astype(jnp.float32)))
               + lambda_init)
        y_d = diff_attention(q, k, v, lam, lambda_init, diff_subln[l]) * jax.nn.silu(g_d)

        mem_n = rmsnorm(mem, norm_mem[l])
        y_x = memory_attention(q_x.reshape(bsz, seq, XATTN_HEADS, XATTN_HEAD_DIM), mem_n, w_mem_kv[l])
        y_x = y_x * jax.nn.silu(g_x)

        mix = jnp.concatenate([y_s, y_d, y_x], axis=-1) @ w_out[l]
        x = x + rmsnorm(mix, norm_post[l])
    return x
```

```python
import math
import numpy as np
import concourse.bass as bass
import concourse.mybir as mybir
from concourse.bass_utils import run_bass_kernel_spmd

F32 = mybir.dt.float32
BF16 = mybir.dt.bfloat16
I32 = mybir.dt.int32
AF = mybir.ActivationFunctionType
ALU = mybir.AluOpType

NCORES = 8
D = 2048
SEQ = 8192
BATCH = 2
TOK = 2048
TB = 512
NTB = TOK // TB
KC = D // 128
EPS = 1e-6
PI = math.pi


class Prog:
    def __init__(self, nc):
        self.nc = nc
        self.eng = {"pe": nc.tensor, "act": nc.scalar, "dve": nc.vector, "pool": nc.gpsimd,
                    "sp": nc.sync}
        self.sem = {e: nc.alloc_semaphore("s_" + e) for e in ("pe", "act", "dve", "pool")}
        self.cnt = {e: 0 for e in self.sem}
        self.seen = {}
        self.nsem = 0
        self.names = 0
        self.stack = None
        self.tag = ""

    def wait(self, e, deps):
        for ev in deps:
            if ev is None:
                continue
            sem, val, key = ev
            if self.seen.get((e, key), 0) >= val:
                continue
            self.eng[e].wait_ge(sem, val)
            self.seen[(e, key)] = val

    def op(self, e, fn, deps=()):
        self.wait(e, deps)
        ins = fn(self.eng[e])
        self.cnt[e] += 1
        ins.then_inc(self.sem[e], 1)
        return (self.sem[e], self.cnt[e], e)

    def dsem(self):
        self.nsem += 1
        return {"sem": self.nc.alloc_semaphore("d%d" % self.nsem), "val": 0, "key": "d%d" % self.nsem}

    def dma(self, q, out, in_, ds, deps=(), **kw):
        self.wait(q, deps)
        ins = self.eng[q].dma_start(out=out, in_=in_, **kw)
        ds["val"] += 16
        ins.then_inc(ds["sem"], 16)
        return (ds["sem"], ds["val"], ds["key"])

    def sb(self, shape, dt, name=None):
        self.names += 1
        nm = self.tag + (name or ("t%d" % self.names))
        if self.stack is not None:
            return self.stack.enter_context(self.nc.sbuf_tensor(nm, list(shape), dt)).ap()
        return self.nc.alloc_sbuf_tensor(nm, list(shape), dt).ap()

    def ps(self, name=None):
        self.names += 1
        nm = self.tag + (name or ("p%d" % self.names))
        if self.stack is not None:
            return self.stack.enter_context(self.nc.psum_tensor(nm, [128, 512], F32)).ap()
        return self.nc.alloc_psum_tensor(nm, [128, 512], F32).ap()

    def begin_phase(self, tag):
        from contextlib import ExitStack
        self.tag = tag
        self.stack = ExitStack()

    def barrier(self, extra=()):
        evs = [(self.sem[e], self.cnt[e], e) for e in self.sem if self.cnt[e] > 0] + list(extra)
        for e in ("pe", "act", "dve", "pool", "sp"):
            self.wait(e, evs)

    def end_phase(self, extra=()):
        self.barrier(extra)
        self.stack.close()
        self.stack = None
        self.tag = ""


def wrap_sin(P, e, out, y, tmp, deps=()):
    ev = P.op(e, lambda g: g.tensor_scalar(out=tmp, in0=y, scalar1=PI, scalar2=-2 * PI,
                                           op0=ALU.is_gt, op1=ALU.mult), deps)
    ev = P.op(e, lambda g: g.tensor_tensor(out=y, in0=y, in1=tmp, op=ALU.add), [ev])
    ev = P.op(e, lambda g: g.tensor_scalar(out=tmp, in0=y, scalar1=-PI, scalar2=2 * PI,
                                           op0=ALU.is_lt, op1=ALU.mult), [ev])
    ev = P.op(e, lambda g: g.tensor_tensor(out=y, in0=y, in1=tmp, op=ALU.add), [ev])
    ev = P.op(e, lambda g: g.tensor_scalar(out=y, in0=y, scalar1=PI, scalar2=-PI,
                                           op0=ALU.min, op1=ALU.max), [ev])
    ev = P.op("act", lambda g: g.activation(out=out, in_=y, func=AF.Sin), [ev])
    return ev


def _kind(ci):
    if ci >= 44:
        return "mv"
    if ci >= 40:
        return "mk"
    return ("u", "gs", "q", "k", "v", "gd", "qx", "gx")[(0, 0, 1, 1, 2, 3, 4, 5, 6, 7)[ci // 4]] \
        if False else (["u"] * 8 + ["gs"] * 8 + ["q"] * 4 + ["k"] * 4 + ["v"] * 4 + ["gd"] * 4
                       + ["qx"] * 4 + ["gx"] * 4)[ci]


A_ORDER = list(range(40, 48)) + list(range(0, 32)) + [36, 32, 37, 33, 38, 34, 39, 35]


def emit_phase_a(nc, P, io):
    xT, wall, gpre, gmem, memT, posr, ropec, perm = (io[k] for k in
        ("xT", "wall", "gpre", "gmem", "memT", "posr", "ropec", "perm"))
    o_u, o_sgs, o_q, o_k, o_v, o_sgd, o_yx = (io[k] for k in
        ("o_u", "o_sgs", "o_q", "o_k", "o_v", "o_sgd", "o_yx"))

    hT = P.sb([128, KC, TOK], BF16, "hT")
    xs = [P.sb([128, 4, TB], F32, "xs%d" % i) for i in range(2)]
    sq = [P.sb([128, 4, TB], BF16, "sq%d" % i) for i in range(2)]
    rstd = P.sb([128, TOK], F32, "rstd")
    rstm = P.sb([128, 256], F32, "rstm")
    rtmp = P.sb([128, TB], F32, "rtmp")
    wf = [P.sb([128, KC, 128], F32, "wf%d" % i) for i in range(2)]
    wb = [P.sb([128, KC, 128], BF16, "wb%d" % i) for i in range(2)]
    outF = [P.sb([128, TOK], F32, "outF%d" % i) for i in range(2)]
    outB = [P.sb([128, TOK], BF16, "outB%d" % i) for i in range(2)]
    sgx = P.sb([128, TOK], F32, "sgx")
    cosT = P.sb([128, TOK], F32, "cosT")
    sinT = P.sb([128, TOK], F32, "sinT")
    memn = P.sb([128, KC, 256], BF16, "memn")
    mkT = P.sb([128, 4, 256], BF16, "mkT")
    mv = P.sb([128, 2, 512], BF16, "mv")
    ones = P.sb([128, 128], BF16, "ones")
    permS = P.sb([128, 128], F32, "permS")
    gpreS = P.sb([128, KC], F32, "gpreS")
    gmemS = P.sb([128, KC], F32, "gmemS")
    ropeS = P.sb([128, 2], F32, "ropeS")
    qf = [P.sb([128, TB], F32, "qf%d" % i) for i in range(2)]
    t1 = [P.sb([128, TB], F32, "t1%d" % i) for i in range(2)]
    t2 = [P.sb([128, TB], F32, "t2%d" % i) for i in range(2)]
    qxb = [P.sb([128, TB], BF16, "qxb%d" % i) for i in range(2)]
    PT = [P.sb([128, 2, TB], BF16, "PT%d" % i) for i in range(2)]
    rc = [P.sb([128, TB], F32, "rc%d" % i) for i in range(2)]

    psA = [P.ps("psA0"), P.ps("psA1")]
    psSS = P.ps("psSS")
    psR = P.ps("psR")
    psS = [P.ps("psS0"), P.ps("psS1")]
    psO = P.ps("psO")
    psL = P.ps("psL")

    dconst = P.dsem()
    ev_c = None
    for dst, src in ((permS, perm), (gpreS, gpre), (gmemS, gmem), (ropeS, ropec)):
        ev_c = P.dma("sp", dst[:], src[:], dconst)
    posI = sgx.bitcast(I32)
    ev_pos = P.dma("sp", posI[:], posr[:], dconst)
    ev_c = ev_pos
    ev_ones = P.op("dve", lambda g: g.memset(ones[:], 1.0))

    tA, tB_ = outF[0], outF[1]
    e = P.op("dve", lambda g: g.tensor_copy(out=tA[:], in_=posI[:]), [ev_pos])
    e = P.op("dve", lambda g: g.tensor_scalar(out=tA[:], in0=tA[:], scalar1=ropeS[:, 0:1], scalar2=None,
                                              op0=ALU.mult), [e])
    uI = outB[0].bitcast(I32) if False else None
    iI = cosT.bitcast(I32)
    e = P.op("dve", lambda g: g.tensor_copy(out=iI[:], in_=tA[:]), [e])
    e = P.op("dve", lambda g: g.tensor_copy(out=tB_[:], in_=iI[:]), [e])
    e = P.op("dve", lambda g: g.tensor_tensor(out=tA[:], in0=tA[:], in1=tB_[:], op=ALU.subtract), [e])
    e_s = P.op("dve", lambda g: g.tensor_scalar(out=tB_[:], in0=tA[:], scalar1=2 * PI, scalar2=None,
                                                op0=ALU.mult), [e])
    e_c = P.op("dve", lambda g: g.tensor_scalar(out=tA[:], in0=tA[:], scalar1=2 * PI, scalar2=PI / 2,
                                                op0=ALU.mult, op1=ALU.add), [e_s])
    scr = sgx
    e1 = wrap_sin(P, "dve", cosT[:], tA[:], scr[:], [e_c])
    e2 = wrap_sin(P, "dve", sinT[:], tB_[:], scr[:], [e_s, e1])
    e2 = P.op("dve", lambda g: g.tensor_scalar(out=sinT[:], in0=sinT[:], scalar1=ropeS[:, 1:2], scalar2=None,
                                               op0=ALU.mult), [e2])
    ev_rope = e2
    ev_setup_bufs = e2

    ld_sem = [P.dsem(), P.dsem()]
    xs_free = [None, None]
    sq_free = [None, None]
    ss_free = [None]
    pieces = []
    for pas in range(2):
        for kg in range(4):
            pieces.append(("m", pas, 0, kg))
    for pas in range(2):
        for tb in range(NTB):
            for kg in range(4):
                pieces.append(("x", pas, tb, kg))
    ev_rstd = {}
    ev_h_last = {}
    for i, (wh, pas, tb, kg) in enumerate(pieces):
        b = i % 2
        if wh == "m":
            n = 256
            src = memT[:, kg * 4:(kg + 1) * 4, :]
            gS, rs, dst = gmemS, rstm[:, 0:256], memn
            c0 = 0
        else:
            n = TB
            src = xT[:, kg * 4:(kg + 1) * 4, tb * TB:(tb + 1) * TB]
            gS, rs, dst = gpreS, rstd[:, tb * TB:(tb + 1) * TB], hT
            c0 = tb * TB
        ev_ld = P.dma("sp", xs[b][:, :, 0:n], src, ld_sem[b], [xs_free[b]])
        if pas == 0:
            ev_sq = P.op("act", lambda g, b=b, n=n: g.activation(out=sq[b][:, :, 0:n], in_=xs[b][:, :, 0:n],
                                                                   func=AF.Square), [ev_ld, sq_free[b]])
            xs_free[b] = ev_sq
            for j in range(4):
                first = (kg == 0 and j == 0)
                last = (kg == 3 and j == 3)
                ev_mm = P.op("pe", lambda g, b=b, j=j, n=n, first=first, last=last:
                             g.matmul(psSS[:, 0:n], lhsT=ones[:], rhs=sq[b][:, j, 0:n], start=first, stop=last),
                             [ev_sq, ev_ones] + ([ss_free[0]] if first else []))
            sq_free[b] = ev_mm
            if kg == 3:
                e = P.op("act", lambda g, n=n: g.activation(out=rtmp[:, 0:n], in_=psSS[:, 0:n], func=AF.Sqrt,
                                                           bias=EPS, scale=1.0 / D), [ev_mm])
                ss_free[0] = e
                e = P.op("dve", lambda g, n=n, rs=rs: g.reciprocal(out=rs, in_=rtmp[:, 0:n]), [e])
                ev_rstd[(wh, tb)] = e
        else:
            for j in range(4):
                kc = kg * 4 + j
                e = P.op("dve", lambda g, b=b, j=j, n=n, kc=kc, gS=gS, rs=rs, dst=dst, c0=c0:
                         g.scalar_tensor_tensor(out=dst[:, kc, c0:c0 + n], in0=xs[b][:, j, 0:n],
                                                scalar=gS[:, kc:kc + 1], in1=rs, op0=ALU.mult, op1=ALU.mult),
                         [ev_ld, ev_rstd[(wh, tb)], ev_c])
            xs_free[b] = e
            ev_h_last[(wh, tb)] = e
    ev_memn = ev_h_last[("m", 0)]

    NCH = len(A_ORDER)
    w_sem = [P.dsem(), P.dsem()]
    st_sem = [P.dsem(), P.dsem()]
    ev_wld = [None] * NCH
    ev_cast = [None] * NCH
    ev_pe_done = [None] * NCH
    ob_free = [ev_setup_bufs, ev_setup_bufs]
    psA_free = [None, None]
    cnt = {"acc": 0, "rope": 0, "ma": 0}
    free_evs = {"qf": [None, None], "t1": [None, None], "t2": [None, None], "psR": None,
                "qxb": [None, None], "PT": [None, None], "psS": [None, None], "psO": None, "psL": None,
                "rc": [None, None]}
    pending = []
    finals = []

    def load_w(n):
        ev_wld[n] = P.dma("sp", wf[n % 2][:], wall[A_ORDER[n]], w_sem[n % 2],
                          [ev_cast[n - 2]] if n >= 2 else [])

    def cast_w(n):
        ev_cast[n] = P.op("pool", lambda g, n=n: g.tensor_copy(out=wb[n % 2][:], in_=wf[n % 2][:]),
                          [ev_wld[n]] + ([ev_pe_done[n - 2]] if n >= 2 else []))

    load_w(0)
    load_w(1)
    cast_w(0)
    SCALE_X = 128 ** -0.5
    on_store = io.get("on_store", lambda gid, ev: None)
    tick = io.get("tick", lambda: None)
    for n in range(NCH):
        ci = A_ORDER[n]
        kind = _kind(ci)
        tick()
        if n + 1 < NCH:
            cast_w(n + 1)
        if n + 2 < NCH:
            load_w(n + 2)
        wbn = wb[n % 2]
        ob = n % 2
        if kind == "mk":
            h = ci - 40
            a = cnt["acc"] % 2
            cnt["acc"] += 1
            for kc in range(KC):
                e = P.op("pe", lambda g, kc=kc, a=a: g.matmul(psA[a][:, 0:256], lhsT=wbn[:, kc, :], rhs=memn[:, kc, :],
                                                               start=(kc == 0), stop=(kc == KC - 1)),
                         [ev_cast[n], ev_memn] + ([psA_free[a]] if kc == 0 else []))
            ev_pe_done[n] = e
            psA_free[a] = P.op("act", lambda g, a=a, h=h: g.activation(out=mkT[:, h, :], in_=psA[a][:, 0:256],
                                                                        func=AF.Copy), [e])
            ev_mk = psA_free[a]
            continue
        if kind == "mv":
            h = ci - 44
            a = cnt["acc"] % 2
            cnt["acc"] += 1
            for mc in range(2):
                for kc in range(KC):
                    e = P.op("pe", lambda g, kc=kc, a=a, mc=mc:
                             g.matmul(psA[a][:, mc * 128:(mc + 1) * 128], lhsT=memn[:, kc, mc * 128:(mc + 1) * 128],
                                      rhs=wbn[:, kc, :], start=(kc == 0), stop=(kc == KC - 1)),
                             [ev_cast[n], ev_memn] + ([psA_free[a]] if (kc == 0 and mc == 0) else []))
            ev_pe_done[n] = e
            for mc in range(2):
                psA_free[a] = P.op("act", lambda g, a=a, h=h, mc=mc:
                                   g.activation(out=mv[:, mc, h * 128:(h + 1) * 128],
                                                in_=psA[a][:, mc * 128:(mc + 1) * 128], func=AF.Copy), [e])
            ev_mv = psA_free[a]
            continue
        if kind == "v":
            h = ci - 24
            vb = outB[ob].rearrange("p (t c) -> p t c", c=128)
            ev_last = None
            for tg in range(4):
                a = cnt["acc"] % 2
                cnt["acc"] += 1
                for j in range(4):
                    tt = tg * 4 + j
                    for kc in range(KC):
                        e = P.op("pe", lambda g, kc=kc, a=a, j=j, tt=tt:
                                 g.matmul(psA[a][:, j * 128:(j + 1) * 128], lhsT=hT[:, kc, tt * 128:(tt + 1) * 128],
                                          rhs=wbn[:, kc, :], start=(kc == 0), stop=(kc == KC - 1)),
                                 [ev_cast[n], ev_h_last[("x", tt // 4)]] +
                                 ([psA_free[a]] if (kc == 0 and j == 0) else []))
                for f in pending:
                    f()
                pending = []
                ev_last = P.op("act", lambda g, a=a, tg=tg:
                               g.activation(out=outB[ob][:, tg * 512:(tg + 1) * 512], in_=psA[a][:], func=AF.Copy),
                               [e, ob_free[ob]])
                psA_free[a] = ev_last
            ev_pe_done[n] = e
            ob_free[ob] = P.dma("sp", o_v[h], vb, st_sem[ob], [ev_last])
            finals.append(ob_free[ob])
            on_store(16 + h, ob_free[ob])
            continue
        ev_out_last = None
        for tb in range(NTB):
            a = cnt["acc"] % 2
            cnt["acc"] += 1
            c0, c1 = tb * TB, (tb + 1) * TB
            for kc in range(KC):
                e = P.op("pe", lambda g, kc=kc, a=a, c0=c0, c1=c1:
                         g.matmul(psA[a][:], lhsT=wbn[:, kc, :], rhs=hT[:, kc, c0:c1],
                                  start=(kc == 0), stop=(kc == KC - 1)),
                         [ev_cast[n], ev_h_last[("x", tb)]] + ([psA_free[a]] if kc == 0 else []))
            ev_mm = e
            if tb == NTB - 1:
                ev_pe_done[n] = e
            for f in pending:
                f()
            pending = []
            if kind == "u":
                ev = P.op("act", lambda g, a=a, c0=c0, c1=c1: g.activation(out=outB[ob][:, c0:c1], in_=psA[a][:],
                                                                           func=AF.Copy), [ev_mm, ob_free[ob]])
                psA_free[a] = ev
                ev_out_last = ev
            elif kind in ("gs", "gd", "gx"):
                dst = sgx if kind == "gx" else outF[ob]
                deps = [ev_mm] + ([ob_free[ob]] if kind != "gx" else [ev_setup_bufs, free_evs.get("sgx")])
                ev = P.op("act", lambda g, a=a, c0=c0, c1=c1, dst=dst: g.activation(out=dst[:, c0:c1], in_=psA[a][:],
                                                                                    func=AF.Silu), deps)
                psA_free[a] = ev
                ev_out_last = ev
                if kind == "gx":
                    ev_sgx = ev
            elif kind in ("q", "k"):
                r = cnt["rope"] % 2
                cnt["rope"] += 1
                ev = P.op("act", lambda g, a=a, r=r: g.activation(out=qf[r][:], in_=psA[a][:], func=AF.Copy),
                          [ev_mm, free_evs["qf"][r]])
                psA_free[a] = ev

                def rope_tail(r=r, ev=ev, c0=c0, c1=c1, ob=ob, last=(tb == NTB - 1), ci=ci, kind=kind):
                    nonlocal ev_out_last
                    e_p = P.op("pe", lambda g: g.matmul(psR[:], lhsT=permS[:], rhs=qf[r][:], start=True, stop=True),
                               [ev, ev_c, free_evs["psR"]])
                    e_1 = P.op("pool", lambda g: g.tensor_tensor(out=t1[r][:], in0=qf[r][:], in1=cosT[:, c0:c1],
                                                                 op=ALU.mult), [ev, ev_rope, free_evs["t1"][r]])
                    e_2 = P.op("dve", lambda g: g.tensor_tensor(out=t2[r][:], in0=psR[:], in1=sinT[:, c0:c1],
                                                                op=ALU.mult), [e_p, ev_rope, free_evs["t2"][r]])
                    free_evs["psR"] = e_2
                    e_3 = P.op("pool", lambda g: g.tensor_tensor(out=outB[ob][:, c0:c1], in0=t1[r][:], in1=t2[r][:],
                                                                 op=ALU.add), [e_1, e_2, ob_free[ob]])
                    free_evs["qf"][r] = e_1 if True else None
                    free_evs["qf"][r] = e_3
                    free_evs["t1"][r] = e_3
                    free_evs["t2"][r] = e_3
                    if last:
                        dstd = (o_q if kind == "q" else o_k)[ci - (16 if kind == "q" else 20)]
                        ob_free[ob] = P.dma("sp", dstd, outB[ob][:], st_sem[ob], [e_3])
                        finals.append(ob_free[ob])
                        on_store((8 if kind == "q" else 12) + ci - (16 if kind == "q" else 20), ob_free[ob])
                pending.append(rope_tail)
            elif kind == "qx":
                h = ci - 32
                r = cnt["ma"] % 2
                cnt["ma"] += 1
                ev = P.op("act", lambda g, a=a, r=r: g.activation(out=qxb[r][:], in_=psA[a][:], func=AF.Copy),
                          [ev_mm, free_evs["qxb"][r]])
                psA_free[a] = ev

                def ma_tail(r=r, ev=ev, c0=c0, c1=c1, ob=ob, last=(tb == NTB - 1), h=h):
                    e_s = []
                    for mc in range(2):
                        e = P.op("pe", lambda g, mc=mc: g.matmul(psS[mc][:], lhsT=mkT[:, h, mc * 128:(mc + 1) * 128],
                                                                 rhs=qxb[r][:], start=True, stop=True),
                                 [ev, ev_mk, free_evs["psS"][mc]])
                        e_s.append(e)
                    free_evs["qxb"][r] = e_s[1]
                    e_x = []
                    for mc in range(2):
                        e = P.op("act", lambda g, mc=mc: g.activation(out=PT[r][:, mc, :], in_=psS[mc][:], func=AF.Exp,
                                                                      scale=SCALE_X),
                                 [e_s[mc], free_evs["PT"][r]])
                        free_evs["psS"][mc] = e
                        e_x.append(e)
                    for mc in range(2):
                        e_o = P.op("pe", lambda g, mc=mc: g.matmul(psO[:], lhsT=mv[:, mc, h * 128:(h + 1) * 128],
                                                                   rhs=PT[r][:, mc, :], start=(mc == 0), stop=(mc == 1)),
                                   [e_x[mc], ev_mv, free_evs["psO"]])
                    for mc in range(2):
                        e_l = P.op("pe", lambda g, mc=mc: g.matmul(psL[:], lhsT=ones[:], rhs=PT[r][:, mc, :],
                                                                   start=(mc == 0), stop=(mc == 1)),
                                   [e_x[mc], free_evs["psL"]])
                    free_evs["PT"][r] = e_l
                    e_r = P.op("dve", lambda g: g.reciprocal(out=rc[r][:], in_=psL[:]), [e_l, free_evs["rc"][r]])
                    free_evs["psL"] = e_r
                    e_m = P.op("dve", lambda g: g.tensor_tensor(out=rc[r][:], in0=psO[:], in1=rc[r][:], op=ALU.mult),
                               [e_o, e_r])
                    free_evs["psO"] = e_m
                    e_f = P.op("pool", lambda g: g.tensor_tensor(out=outB[ob][:, c0:c1], in0=rc[r][:], in1=sgx[:, c0:c1],
                                                                 op=ALU.mult), [e_m, ev_sgx, ob_free[ob]])
                    free_evs["rc"][r] = e_f
                    if last:
                        free_evs["sgx"] = e_f
                        ob_free[ob] = P.dma("sp", o_yx[h], outB[ob][:], st_sem[ob], [e_f])
                        finals.append(ob_free[ob])
                pending.append(ma_tail)
        if kind == "u":
            ob_free[ob] = P.dma("sp", o_u[ci], outB[ob][:], st_sem[ob], [ev_out_last])
            finals.append(ob_free[ob])
            on_store(ci, ob_free[ob])
        elif kind == "gs":
            ob_free[ob] = P.dma("sp", o_sgs[ci - 8], outF[ob][:], st_sem[ob], [ev_out_last])
            finals.append(ob_free[ob])
        elif kind == "gd":
            ob_free[ob] = P.dma("sp", o_sgd[ci - 28], outF[ob][:], st_sem[ob], [ev_out_last])
            finals.append(ob_free[ob])
    for f in pending:
        f()
    pending = []
    return finals


def build_phase_a():
    nc = bass.Bass("TRN2", target_bir_lowering=False)
    P = Prog(nc)
    io = {}
    def din(name, shape, dt=F32):
        io[name] = nc.dram_tensor(name, list(shape), dt, kind="ExternalInput").ap()
    def dout(name, shape, dt):
        io[name] = nc.dram_tensor(name, list(shape), dt, kind="ExternalOutput").ap()
    din("xT", [128, KC, TOK]); din("wall", [48, 128, KC, 128]); din("gpre", [128, KC]); din("gmem", [128, KC])
    din("memT", [128, KC, 256]); din("posr", [128, TOK], I32); din("ropec", [128, 2]); din("perm", [128, 128])
    dout("o_u", [8, 128, TOK], BF16); dout("o_sgs", [8, 128, TOK], F32)
    dout("o_q", [4, 128, TOK], BF16); dout("o_k", [4, 128, TOK], BF16)
    dout("o_v", [4, 128, 16, 128], BF16); dout("o_sgd", [4, 128, TOK], F32); dout("o_yx", [4, 128, TOK], BF16)
    finals = emit_phase_a(nc, P, io)
    P.wait("sp", finals)
    return nc


def rope_consts():
    p = np.arange(128)
    i = (p % 64) % 32
    inv = (10000.0 ** (-(2.0 * i) / 64.0)).astype(np.float32)
    c = np.zeros((128, 2), np.float32)
    c[:, 0] = (inv.astype(np.float64) / (2 * np.pi)).astype(np.float32)
    c[:, 1] = np.where((p % 64) < 32, -1.0, 1.0)
    perm = np.zeros((128, 128), np.float32)
    partner = np.where((p % 64) < 32, p + 32, p - 32)
    perm[partner, p] = 1.0
    return c, perm


def chunkT(a, nrow_chunks):
    R, C = a.shape
    return np.ascontiguousarray(a.reshape(R // 128, 128, C).transpose(1, 0, 2))


def wchunks(w):
    R, C = w.shape
    return np.ascontiguousarray(w.reshape(R // 128, 128, C // 128, 128).transpose(2, 1, 0, 3))


def phase_a_inputs(inputs, l, x_cur, xT_list=None):
    wall = np.concatenate([wchunks(inputs["w_in"][l]), wchunks(inputs["w_mem_kv"][l])], axis=0)
    gpre = np.ascontiguousarray(inputs["norm_pre"][l].reshape(KC, 128).T)
    gmem = np.ascontiguousarray(inputs["norm_mem"][l].reshape(KC, 128).T)
    ropec, perm = rope_consts()
    maps = []
    for c in range(NCORES):
        b, tq = c // 4, c % 4
        if xT_list is not None:
            xT = xT_list[c]
        else:
            xs = x_cur[b, tq * TOK:(tq + 1) * TOK, :]
            xT = chunkT(np.ascontiguousarray(xs.T), KC)
        memT = chunkT(np.ascontiguousarray(inputs["mem"][b].T), KC)
        posr = np.ascontiguousarray(np.broadcast_to(inputs["positions"][b, tq * TOK:(tq + 1) * TOK][None, :],
                                                    (128, TOK))).astype(np.int32)
        maps.append({"xT": xT, "wall": wall, "gpre": gpre, "gmem": gmem, "memT": memT, "posr": posr,
                     "ropec": ropec, "perm": perm})
    return maps


SBLK = 1024


def emit_phase_s(nc, P, io):
    uT, ys = io.get("uT"), io.get("ys")
    ys_dt = io.get("ys_dt", F32)
    ys_dst = io.get("ys_dst", lambda g_, b, t0: ys[g_ * 16:(g_ + 1) * 16, b, t0:t0 + SBLK])
    lr, li, ldt, BR, BI, CRI, CIR, sg, Dm, bmask, ramp, ident = (io[k] for k in
        ("lr", "li", "ldt", "BR", "BI", "CRI", "CIR", "sg", "Dm", "bmask", "ramp", "ident"))
    S = lambda shape, dt, name: P.sb(shape, dt, name)
    u = S([128, 2, SEQ], BF16, "u")
    cosT = S([128, SEQ], F32, "cosT")
    nsinT = S([128, SEQ], F32, "nsinT")
    tt = S([128, 32, 64], F32, "tt")
    ti = S([128, 32, 64], I32, "ti")
    ab = S([128, 32, 64], F32, "ab")
    bt = [S([128, SBLK], F32, "bt%d" % i) for i in range(2)]
    w = [S([128, SBLK], F32, "w%d" % i) for i in range(2)]
    wc = [S([128, SBLK], BF16, "wc%d" % i) for i in range(2)]
    wsn = [S([128, SBLK], BF16, "wsn%d" % i) for i in range(2)]
    t1 = [S([128, 512], F32, "t1%d" % i) for i in range(2)]
    t2 = [S([128, 512], F32, "t2%d" % i) for i in range(2)]
    ysb = [S([16, SBLK], ys_dt, "ysb%d" % i) for i in range(2)]
    names = ("lr", "li", "ldt")
    prm = {k: S([128, 8], F32, "p_" + k) for k in names}
    tmp = {k: S([128, 8], F32, "q_" + k) for k in
           ("dt", "lrdt", "rmag", "f", "fr", "fa", "sn", "cs", "are", "aim", "nr", "den", "rden", "zre", "zim",
            "a", "b", "szim", "szre", "nzim", "f64", "F64")}
    tmpi = S([128, 8], I32, "tmpi")
    BRs, BIs, CRIs, CIRs, Dms = (S([128, 8, 16], F32, n) for n in ("BRs", "BIs", "CRIs", "CIRs", "Dms"))
    sgS = S([128, 1], F32, "sgS")
    nsgS = S([128, 1], F32, "nsgS")
    bmS = S([128, 8], F32, "bmS")
    rampS = S([128, 128], F32, "rampS")
    identS = S([128, 128], F32, "identS")
    Bc1 = S([128, 8, 16], F32, "Bc1")
    Bc2 = S([128, 8, 16], F32, "Bc2")
    btmp = S([128, 16], F32, "btmp")
    W1T = S([128, 128], F32, "W1T")
    W2T = S([128, 128], F32, "W2T")
    L1 = S([128, 8, 128], BF16, "L1")
    L2 = S([128, 8, 128], BF16, "L2")
    Cc = S([128, 8, 16], BF16, "Cc")
    Cs = S([128, 8, 16], BF16, "Cs")
    Db = S([128, 8, 16], BF16, "Db")
    gc = S([128, 8, 128], F32, "gc")
    gci = S([128, 8, 128], I32, "gci")
    hj = S([128, 8, 64], F32, "hj")
    hji = S([128, 8, 64], I32, "hji")

    psX1 = [P.ps("psX10"), P.ps("psX11")]
    psX2 = [P.ps("psX20"), P.ps("psX21")]
    psY = [P.ps("psY0"), P.ps("psY1")]
    psT = P.ps("psT")

    dc = P.dsem()
    ev = None
    for dst, src in ((prm["lr"], lr), (prm["li"], li), (prm["ldt"], ldt), (BRs, BR), (BIs, BI), (CRIs, CRI),
                     (CIRs, CIR), (sgS, sg), (Dms, Dm), (bmS, bmask), (rampS, ramp), (identS, ident)):
        ev = P.dma("sp", dst[:], src[:], dc)
    ev_c = ev
    du = P.dsem()
    ev_u = None
    if "u_loader" in io:
        ev_u = io["u_loader"](u, du)
    else:
        for b in range(2):
            for hq in range(2):
                ev_u = P.dma("sp", u[:, b, hq * 4096:(hq + 1) * 4096], uT[:, b, hq * 4096:(hq + 1) * 4096], du)

    T = tmp
    last = [ev_c]

    def V(fn, eng="dve"):
        last[0] = P.op(eng, fn, [last[0]])
        return last[0]

    def frac(dst, src, itmp):
        V(lambda g: g.tensor_copy(out=itmp, in_=src))
        V(lambda g: g.tensor_tensor(out=dst, in0=src, in1=itmp, op=ALU.subtract))

    TT = lambda o, a, b, op: V(lambda g: g.tensor_tensor(out=o, in0=a, in1=b, op=op))
    V(lambda g: g.activation(out=T["dt"][:], in_=prm["ldt"][:], func=AF.Exp), "act")
    TT(T["lrdt"][:], prm["lr"][:], T["dt"][:], ALU.mult)
    V(lambda g: g.activation(out=T["rmag"][:], in_=T["lrdt"][:], func=AF.Exp), "act")
    TT(T["f"][:], prm["li"][:], T["dt"][:], ALU.mult)
    V(lambda g: g.tensor_scalar(out=T["f"][:], in0=T["f"][:], scalar1=1.0 / (2 * PI), scalar2=None, op0=ALU.mult))
    frac(T["fr"][:], T["f"][:], tmpi[:])
    V(lambda g: g.activation(out=T["sn"][:], in_=T["fr"][:], func=AF.Sin, scale=2 * PI), "act")
    V(lambda g: g.activation(out=T["fa"][:], in_=T["fr"][:], func=AF.Abs), "act")
    V(lambda g: g.activation(out=T["cs"][:], in_=T["fa"][:], func=AF.Sin, scale=-2 * PI, bias=PI / 2), "act")
    TT(T["are"][:], T["rmag"][:], T["cs"][:], ALU.mult)
    TT(T["aim"][:], T["rmag"][:], T["sn"][:], ALU.mult)
    V(lambda g: g.tensor_scalar(out=T["nr"][:], in0=T["are"][:], scalar1=-1.0, scalar2=None, op0=ALU.add))
    TT(T["a"][:], prm["lr"][:], prm["lr"][:], ALU.mult)
    TT(T["b"][:], prm["li"][:], prm["li"][:], ALU.mult)
    TT(T["den"][:], T["a"][:], T["b"][:], ALU.add)
    V(lambda g: g.reciprocal(out=T["rden"][:], in_=T["den"][:]))
    TT(T["a"][:], T["nr"][:], prm["lr"][:], ALU.mult)
    TT(T["b"][:], T["aim"][:], prm["li"][:], ALU.mult)
    TT(T["zre"][:], T["a"][:], T["b"][:], ALU.add)
    TT(T["zre"][:], T["zre"][:], T["rden"][:], ALU.mult)
    TT(T["a"][:], T["aim"][:], prm["lr"][:], ALU.mult)
    TT(T["b"][:], T["nr"][:], prm["li"][:], ALU.mult)
    TT(T["zim"][:], T["a"][:], T["b"][:], ALU.subtract)
    TT(T["zim"][:], T["zim"][:], T["rden"][:], ALU.mult)
    V(lambda g: g.tensor_scalar(out=T["szim"][:], in0=T["zim"][:], scalar1=sgS[:, 0:1], scalar2=None, op0=ALU.mult))
    V(lambda g: g.tensor_scalar(out=T["szre"][:], in0=T["zre"][:], scalar1=sgS[:, 0:1], scalar2=None, op0=ALU.mult))
    V(lambda g: g.tensor_scalar(out=T["nzim"][:], in0=T["zim"][:], scalar1=-1.0, scalar2=None, op0=ALU.mult))
    V(lambda g: g.tensor_scalar(out=nsgS[:], in0=sgS[:], scalar1=-1.0, scalar2=None, op0=ALU.mult))
    for g_ in range(8):
        sl = slice(g_, g_ + 1)
        V(lambda g, g_=g_, sl=sl: g.tensor_scalar(out=btmp[:], in0=BIs[:, g_, :], scalar1=T["szim"][:, sl], scalar2=None,
                                                  op0=ALU.mult))
        V(lambda g, g_=g_, sl=sl: g.scalar_tensor_tensor(out=Bc1[:, g_, :], in0=BRs[:, g_, :], scalar=T["zre"][:, sl],
                                                         in1=btmp[:], op0=ALU.mult, op1=ALU.add))
        V(lambda g, g_=g_, sl=sl: g.tensor_scalar(out=btmp[:], in0=BRs[:, g_, :], scalar1=T["nzim"][:, sl], scalar2=None,
                                                  op0=ALU.mult))
        V(lambda g, g_=g_, sl=sl: g.scalar_tensor_tensor(out=Bc2[:, g_, :], in0=BIs[:, g_, :], scalar=T["szre"][:, sl],
                                                         in1=btmp[:], op0=ALU.mult, op1=ALU.add))
    for Bc, WT, L in ((Bc1, W1T, L1), (Bc2, W2T, L2)):
        V(lambda g, Bc=Bc: g.transpose(psT[:, 0:128], Bc.rearrange("p g h -> p (g h)"), identS[:]), "pe")
        V(lambda g, WT=WT: g.tensor_copy(out=WT[:], in_=psT[:, 0:128]))
        for g_ in range(8):
            V(lambda g, g_=g_, WT=WT, L=L: g.tensor_scalar(out=L[:, g_, :], in0=WT[:], scalar1=bmS[:, g_:g_ + 1],
                                                           scalar2=None, op0=ALU.mult))
    V(lambda g: g.tensor_scalar(out=Cc.rearrange("p g h -> p (g h)"), in0=CRIs.rearrange("p g h -> p (g h)"),
                                scalar1=nsgS[:, 0:1], scalar2=None, op0=ALU.mult))
    V(lambda g: g.tensor_copy(out=Cs.rearrange("p g h -> p (g h)"), in_=CIRs.rearrange("p g h -> p (g h)")))
    V(lambda g: g.tensor_copy(out=Db.rearrange("p g h -> p (g h)"), in_=Dms.rearrange("p g h -> p (g h)")))
    V(lambda g: g.tensor_scalar(out=T["f64"][:], in0=T["f"][:], scalar1=64.0, scalar2=None, op0=ALU.mult))
    frac(T["F64"][:], T["f64"][:], tmpi[:])
    for g_ in range(8):
        V(lambda g, g_=g_: g.tensor_scalar(out=gc[:, g_, :], in0=rampS[:, 0:128], scalar1=T["F64"][:, g_:g_ + 1],
                                           scalar2=None, op0=ALU.mult))
        V(lambda g, g_=g_: g.tensor_scalar(out=hj[:, g_, :], in0=rampS[:, 0:64], scalar1=T["f"][:, g_:g_ + 1],
                                           scalar2=None, op0=ALU.mult))
    frac(gc[:], gc[:], gci[:])
    frac(hj[:], hj[:], hji[:])
    ev_prm = last[0]

    st = [P.dsem(), P.dsem()]
    finals = []
    tab_free = None
    tmp_free = {"tt": None, "ti": None, "ab": None}
    fr = {"psX1": [None, None], "psX2": [None, None], "t1": [None, None], "t2": [None, None],
          "bt": [None, None], "w": [None, None], "wc": [None, None], "wsn": [None, None],
          "psY": [None, None], "ysb": [None, None]}
    pending = []
    it = 0
    sub = 0
    ev_scan_prev = None
    NB = SEQ // SBLK
    for g_ in range(8):
        for q in range(4):
            qs = slice(q * 2048, (q + 1) * 2048)
            e = P.op("pool", lambda g, g_=g_, q=q: g.tensor_tensor(
                out=tt[:], in0=gc[:, g_, q * 32:(q + 1) * 32].unsqueeze(2).to_broadcast([128, 32, 64]),
                in1=hj[:, g_, :].unsqueeze(1).to_broadcast([128, 32, 64]), op=ALU.add), [ev_prm, tmp_free["tt"]])
            e2 = P.op("dve", lambda g: g.tensor_copy(out=ti[:], in_=tt[:]), [e, tmp_free["ti"]])
            e3 = P.op("pool", lambda g: g.tensor_tensor(out=tt[:], in0=tt[:], in1=ti[:], op=ALU.subtract), [e2])
            tmp_free["ti"] = e3
            e4 = P.op("act", lambda g, qs=qs: g.activation(out=nsinT[:, qs], in_=tt.rearrange("p a b -> p (a b)"),
                                                           func=AF.Sin, scale=-2 * PI), [e3, tab_free])
            e5 = P.op("act", lambda g: g.activation(out=ab.rearrange("p a b -> p (a b)"),
                                                    in_=tt.rearrange("p a b -> p (a b)"), func=AF.Abs),
                      [e3, tmp_free["ab"]])
            tmp_free["tt"] = e5
            e6 = P.op("act", lambda g, qs=qs: g.activation(out=cosT[:, qs], in_=ab.rearrange("p a b -> p (a b)"),
                                                           func=AF.Sin, scale=-2 * PI, bias=PI / 2), [e5, tab_free])
            tmp_free["ab"] = e6
        ev_tab = e6
        rbc = T["rmag"][:, g_:g_ + 1].to_broadcast([128, SBLK])
        for b in range(2):
            for tb in range(NB):
                i = it % 2
                it += 1
                t0 = tb * SBLK
                e_add = None
                for s in range(SBLK // 512):
                    x = sub % 2
                    sub += 1
                    cs = slice(t0 + s * 512, t0 + (s + 1) * 512)
                    e1 = P.op("pe", lambda g, x=x, cs=cs: g.matmul(psX1[x][:], lhsT=L1[:, g_, :], rhs=u[:, b, cs],
                                                                   start=True, stop=True),
                              [ev_prm, ev_u, fr["psX1"][x]])
                    e2 = P.op("pe", lambda g, x=x, cs=cs: g.matmul(psX2[x][:], lhsT=L2[:, g_, :], rhs=u[:, b, cs],
                                                                   start=True, stop=True),
                              [ev_prm, ev_u, fr["psX2"][x]])
                    if s == 0:
                        for f in pending:
                            f()
                        pending = []
                    d1 = P.op("dve", lambda g, x=x, cs=cs: g.tensor_tensor(out=t1[x][:], in0=psX1[x][:], in1=cosT[:, cs],
                                                                           op=ALU.mult), [e1, ev_tab, fr["t1"][x]])
                    fr["psX1"][x] = d1
                    d2 = P.op("dve", lambda g, x=x, cs=cs: g.tensor_tensor(out=t2[x][:], in0=psX2[x][:], in1=nsinT[:, cs],
                                                                           op=ALU.mult), [e2, ev_tab, fr["t2"][x]])
                    fr["psX2"][x] = d2
                    e_add = P.op("pool", lambda g, x=x, s=s, i=i: g.tensor_tensor(
                        out=bt[i][:, s * 512:(s + 1) * 512], in0=t1[x][:], in1=t2[x][:], op=ALU.add),
                        [d1, d2, fr["bt"][i]])
                    fr["t1"][x] = e_add
                    fr["t2"][x] = e_add
                init = 0.0 if tb == 0 else w[1 - i][:, SBLK - 1:SBLK]
                e_sc = P.op("dve", lambda g, i=i, init=init: g.tensor_tensor_scan(
                    out=w[i][:], data0=rbc, data1=bt[i][:], initial=init, op0=ALU.mult, op1=ALU.add),
                    [e_add, fr["w"][i], ev_scan_prev])
                ev_scan_prev = e_sc
                fr["bt"][i] = e_sc
                ts = slice(t0, t0 + SBLK)
                e_wc = P.op("pool", lambda g, i=i, ts=ts: g.tensor_tensor(out=wc[i][:], in0=w[i][:], in1=cosT[:, ts],
                                                                         op=ALU.mult), [e_sc, fr["wc"][i]])
                e_ws = P.op("pool", lambda g, i=i, ts=ts: g.tensor_tensor(out=wsn[i][:], in0=w[i][:], in1=nsinT[:, ts],
                                                                         op=ALU.mult), [e_sc, fr["wsn"][i]])
                fr["w"][i] = e_ws
                tab_free = e_ws

                def ytail(i=i, b=b, g_=g_, t0=t0, e_wc=e_wc, e_ws=e_ws):
                    ev_last = None
                    for s in range(SBLK // 512):
                        cs = slice(s * 512, (s + 1) * 512)
                        us = slice(t0 + s * 512, t0 + (s + 1) * 512)
                        y = (2 * i + s) % 2
                        P.op("pe", lambda g: g.matmul(psY[y][0:16, :], lhsT=Cc[:, g_, :], rhs=wc[i][:, cs],
                                                      start=True, stop=False), [e_wc, fr["psY"][y]])
                        P.op("pe", lambda g: g.matmul(psY[y][0:16, :], lhsT=Cs[:, g_, :], rhs=wsn[i][:, cs],
                                                      start=False, stop=False), [e_ws])
                        e_y = P.op("pe", lambda g: g.matmul(psY[y][0:16, :], lhsT=Db[:, g_, :], rhs=u[:, b, us],
                                                            start=False, stop=True), [])
                        ev_last = P.op("act", lambda g: g.activation(out=ysb[i][:, cs], in_=psY[y][0:16, :],
                                                                     func=AF.Copy), [e_y, fr["ysb"][i]])
                        fr["psY"][y] = ev_last
                    fr["wc"][i] = e_y
                    fr["wsn"][i] = e_y
                    fr["ysb"][i] = P.dma("sp", ys_dst(g_, b, t0), ysb[i][:], st[i], [ev_last])
                    finals.append(fr["ysb"][i])
                pending.append(ytail)
    for f in pending:
        f()
    return finals


def build_phase_s():
    nc = bass.Bass("TRN2", target_bir_lowering=False)
    P = Prog(nc)
    io = {}
    def din(name, shape, dt=F32):
        io[name] = nc.dram_tensor(name, list(shape), dt, kind="ExternalInput").ap()
    din("uT", [128, 2, SEQ], BF16)
    for k in ("lr", "li", "ldt", "bmask"):
        din(k, [128, 8])
    for k in ("BR", "BI", "CRI", "CIR", "Dm"):
        din(k, [128, 8, 16])
    din("sg", [128, 1]); din("ramp", [128, 128]); din("ident", [128, 128])
    io["ys"] = nc.dram_tensor("ys", [128, 2, SEQ], F32, kind="ExternalOutput").ap()
    finals = emit_phase_s(nc, P, io)
    P.wait("sp", finals)
    return nc


def phase_s_params(inputs, l, c):
    gs = slice(8 * c, 8 * c + 8)
    a_re = inputs["ssm_a_re"][l, gs]
    a_im = inputs["ssm_a_im"][l, gs]
    dup = lambda m: np.ascontiguousarray(np.concatenate([m.T, m.T], axis=0)).astype(np.float32)
    b_re = inputs["ssm_b_re"][l, gs].transpose(1, 0, 2)
    b_im = inputs["ssm_b_im"][l, gs].transpose(1, 0, 2)
    c_re = inputs["ssm_c_re"][l, gs].transpose(2, 0, 1)
    c_im = inputs["ssm_c_im"][l, gs].transpose(2, 0, 1)
    cat = lambda a, b: np.ascontiguousarray(np.concatenate([a, b], axis=0)).astype(np.float32)
    d = inputs["ssm_d"][l, gs]
    Dm = np.zeros((128, 8, 16), np.float32)
    for g in range(8):
        Dm[16 * g + np.arange(16), g, np.arange(16)] = d[g]
    bmask = np.zeros((128, 8), np.float32)
    bmask[np.arange(128), np.arange(128) // 16] = 1.0
    sg = np.concatenate([-np.ones((64, 1)), np.ones((64, 1))]).astype(np.float32)
    return {"lr": dup(a_re), "li": dup(a_im),
            "ldt": np.ascontiguousarray(np.broadcast_to(inputs["ssm_log_dt"][l, gs][None, :], (128, 8))).astype(np.float32),
            "BR": cat(b_re, b_im), "BI": cat(b_im, b_re), "CRI": cat(c_re, c_im), "CIR": cat(c_im, c_re),
            "sg": sg, "Dm": Dm, "bmask": bmask,
            "ramp": np.ascontiguousarray(np.broadcast_to(np.arange(128, dtype=np.float32)[None, :], (128, 128))),
            "ident": np.eye(128, dtype=np.float32)}


QB = 256
NQB = SEQ // QB


def emit_phase_t(nc, P, io, lambda_init):
    qT, kT, Vd, lq, gsub, ident, ydT = (io.get(k) for k in ("qT", "kT", "V", "lq", "gsub", "ident", "ydT"))
    yd_dt = io.get("yd_dt", F32)
    yd_dst = io.get("yd_dst", lambda c0: ydT[:, c0:c0 + 1024])
    q_src = io.get("q_src", lambda i: qT[:, i * 2048:(i + 1) * 2048])
    k_src = io.get("k_src", lambda i: kT[:, i * 2048:(i + 1) * 2048])
    v_src = io.get("v_src", lambda i: Vd[i * 2048:(i + 1) * 2048, :].rearrange("(t p) c -> p t c", p=128))
    ld_deps = io.get("ld_deps", [])
    S = lambda shape, dt, name: P.sb(shape, dt, name)
    q = S([128, SEQ], BF16, "q")
    k = S([128, SEQ], BF16, "k")
    Va = S([128, SEQ // 128, 129], BF16, "Va")
    NPT = 6
    PT = [S([128, QB], BF16, "PT%d" % i) for i in range(NPT)]
    PM = [S([128, 128], BF16, "PM%d" % i) for i in range(4)]
    lqS = S([128, 4, 64], F32, "lqS")
    lprod = S([128, 2, 64], F32, "lprod")
    lsum = S([128, 2], F32, "lsum")
    lexp = S([128, 2], F32, "lexp")
    nlam = S([128, 1], F32, "nlam")
    gsS = S([128, 128], F32, "gsS")
    identS = S([128, 128], F32, "identS")
    rl = [S([128, 4], F32, "rl%d" % i) for i in range(2)]
    oa = [S([128, 128], F32, "oa%d" % i) for i in range(2)]
    ob = [S([128, 128], F32, "ob%d" % i) for i in range(2)]
    junk = S([128, 128], F32, "junk")
    ssq = [S([128, 2], F32, "ssq%d" % i) for i in range(2)]
    outT = [S([128, 1024], yd_dt, "outT%d" % i) for i in range(2)]

    psO = [[P.ps("psO%d%d" % (c, s)) for s in range(2)] for c in range(2)]
    psS = [P.ps("psS%d" % i) for i in range(3)]
    psT = P.ps("psT")

    dc = P.dsem()
    e = P.dma("sp", lqS[:], lq[:], dc)
    e = P.dma("sp", gsS[:], gsub[:], dc)
    e = P.dma("sp", identS[:], ident[:], dc)
    ev_c = e
    dq = P.dsem()
    ev_q = [None] * 4
    v_stage = io.get("v_stage", False)
    if v_stage:
        Vs = [S([128, 2048 + 64], BF16, "Vs%d" % i)[:, 0:2048] for i in range(2)]
        vs_free = [None, None]
    for i in range(4):
        sl = slice(i * 2048, (i + 1) * 2048)
        P.dma("sp", k[:, sl], k_src(i), dq, ld_deps)
        e_qk = P.dma("sp", q[:, sl], q_src(i), dq)
        if v_stage:
            e_v = P.dma("sp", Vs[i % 2][:], v_src(i), dq, [vs_free[i % 2]])
            e_cp = P.op("pool", lambda g, i=i: g.tensor_copy(out=Va[:, i * 16:(i + 1) * 16, 0:128],
                                                            in_=Vs[i % 2].rearrange("p (t d) -> p t d", d=128)), [e_v])
            vs_free[i % 2] = e_cp
            ev_q[i] = (e_v, e_cp)
        else:
            ev_q[i] = (P.dma("sp", Va[:, i * 16:(i + 1) * 16, 0:128], v_src(i), dq), None)
    ev_ones = P.op("pool", lambda g: g.memset(Va[:, :, 128:129], 1.0))
    e = P.op("dve", lambda g: g.tensor_tensor(out=lprod[:, 0, :], in0=lqS[:, 0, :], in1=lqS[:, 1, :], op=ALU.mult), [ev_c])
    e = P.op("dve", lambda g: g.tensor_tensor(out=lprod[:, 1, :], in0=lqS[:, 2, :], in1=lqS[:, 3, :], op=ALU.mult), [e])
    e = P.op("dve", lambda g: g.tensor_reduce(out=lsum[:], in_=lprod[:], axis=mybir.AxisListType.X, op=ALU.add), [e])
    e = P.op("act", lambda g: g.activation(out=lexp[:], in_=lsum[:], func=AF.Exp), [e])
    e = P.op("dve", lambda g: g.tensor_tensor(out=nlam[:], in0=lexp[:, 1:2], in1=lexp[:, 0:1], op=ALU.subtract), [e])
    e = P.op("dve", lambda g: g.tensor_scalar(out=nlam[:], in0=nlam[:], scalar1=-float(lambda_init), scalar2=None,
                                              op0=ALU.add), [e])
    e = P.op("dve", lambda g: g.tensor_scalar(out=gsS[:], in0=gsS[:], scalar1=float(1.0 - lambda_init), scalar2=None,
                                              op0=ALU.mult), [e])
    ev_prm = e

    SC = 64 ** -0.5
    fr = {"psS": [None] * 3, "PT": [None] * NPT, "PM": [None] * 4, "psO": [[None, None], [None, None]],
          "psT": None, "outT": [None, None], "oa": [None, None], "ob": [None, None], "rl": [None, None],
          "ssq": [None, None], "junk": None}
    cnt = {"s": 0, "p": 0, "m": 0, "e": 0}
    st = [P.dsem(), P.dsem()]
    finals = []
    pending = []
    for qb in range(NQB):
        q0 = qb * QB
        nkt = 2 * qb + 2
        evq = ev_q[min(3, (q0 + QB - 1) // 2048)]
        last_pv = [[None, None], [None, None]]
        for kt in range(nkt):
            o = kt - 2 * qb
            c_lo = 128 if o == 1 else 0
            ncol = QB - c_lo
            pts = []
            for c in range(2):
                si = cnt["s"] % 3
                cnt["s"] += 1
                pi = cnt["p"] % NPT
                cnt["p"] += 1
                rows = slice(c * 64, (c + 1) * 64)
                e_s = P.op("pe", lambda g, si=si, rows=rows, kt=kt, c_lo=c_lo, ncol=ncol:
                           g.matmul(psS[si][:, 0:ncol], lhsT=k[rows, kt * 128:(kt + 1) * 128],
                                    rhs=q[rows, q0 + c_lo:q0 + QB], start=True, stop=True),
                           list(evq) + [fr["psS"][si]])
                e_x = P.op("act", lambda g, si=si, pi=pi, c_lo=c_lo, ncol=ncol:
                           g.activation(out=PT[pi][:, c_lo:QB], in_=psS[si][:, 0:ncol], func=AF.Exp, scale=SC),
                           [e_s, fr["PT"][pi]])
                fr["psS"][si] = e_x
                srcs = {}
                if o >= 0:
                    mi = cnt["m"] % 4
                    cnt["m"] += 1
                    e_m = P.op("pool", lambda g, mi=mi, pi=pi, c_lo=c_lo:
                               g.affine_select(out=PM[mi][:], in_=PT[pi][:, c_lo:c_lo + 128], pattern=[[1, 128]],
                                               compare_op=ALU.is_ge, fill=0.0, base=0, channel_multiplier=-1),
                               [e_x, fr["PM"][mi]])
                    srcs[o] = (PM[mi][:], e_m, ("PM", mi))
                    if o == 0:
                        srcs[1] = (PT[pi][:, 128:256], e_x, ("PT", pi))
                else:
                    srcs[0] = (PT[pi][:, 0:128], e_x, ("PT", pi))
                    srcs[1] = (PT[pi][:, 128:256], e_x, ("PT", pi))
                pts.append((srcs, pi))
            if kt == 0:
                pass
            for f in pending:
                f()
            pending = []

            def pv(pts=pts, kt=kt, qb=qb, nkt=nkt, last_pv=last_pv):
                for c in range(2):
                    srcs, pi = pts[c]
                    e_l = None
                    for qs, (ap, e_src, key) in sorted(srcs.items()):
                        first = (kt == 0)
                        lastk = (kt == 2 * qb + qs)
                        e_l = P.op("pe", lambda g, ap=ap, c=c, qs=qs, first=first, lastk=lastk:
                                   g.matmul(psO[c][qs][:, 0:129], lhsT=ap, rhs=Va[:, kt, :], start=first, stop=lastk),
                                   [e_src, ev_ones] + ([fr["psO"][c][qs]] if first else []))
                        last_pv[c][qs] = e_l
                        if key[0] == "PM":
                            fr["PM"][key[1]] = e_l
                    fr["PT"][pi] = e_l
            pending.append(pv)
        for f in pending:
            f()
        pending = []
        ot = (qb // 4) % 2
        for qs in range(2):
            i = cnt["e"] % 2
            cnt["e"] += 1
            e0, e1 = last_pv[0][qs], last_pv[1][qs]
            a = P.op("dve", lambda g, i=i, qs=qs: g.reciprocal(out=rl[i][:, 0:1], in_=psO[0][qs][:, 128:129]),
                     [e0, fr["rl"][i]])
            a = P.op("dve", lambda g, i=i, qs=qs: g.reciprocal(out=rl[i][:, 1:2], in_=psO[1][qs][:, 128:129]), [e1, a])
            a = P.op("dve", lambda g, i=i: g.tensor_tensor(out=rl[i][:, 2:3], in0=rl[i][:, 1:2], in1=nlam[:], op=ALU.mult),
                     [a, ev_prm])
            a = P.op("dve", lambda g, i=i, qs=qs: g.tensor_scalar(out=oa[i][:], in0=psO[0][qs][:, 0:128],
                                                                  scalar1=rl[i][:, 0:1], scalar2=None, op0=ALU.mult),
                     [a, fr["oa"][i]])
            fr["psO"][0][qs] = a
            a = P.op("dve", lambda g, i=i, qs=qs: g.scalar_tensor_tensor(out=ob[i][:], in0=psO[1][qs][:, 0:128],
                                                                         scalar=rl[i][:, 2:3], in1=oa[i][:],
                                                                         op0=ALU.mult, op1=ALU.add), [a, fr["ob"][i]])
            fr["psO"][1][qs] = a
            s1 = P.op("act", lambda g, i=i: g.activation(out=junk[:], in_=ob[i][:], func=AF.Square,
                                                         accum_out=ssq[i][:, 0:1]), [a, fr["junk"], fr["ssq"][i]])
            fr["junk"] = s1
            s2 = P.op("act", lambda g, i=i: g.activation(out=ssq[i][:, 1:2], in_=ssq[i][:, 0:1], func=AF.Sqrt,
                                                         bias=EPS, scale=1.0 / 128), [s1])
            a = P.op("dve", lambda g, i=i: g.reciprocal(out=rl[i][:, 3:4], in_=ssq[i][:, 1:2]), [s2])
            fr["ssq"][i] = a
            a = P.op("dve", lambda g, i=i: g.scalar_tensor_tensor(out=oa[i][:], in0=ob[i][:], scalar=rl[i][:, 3:4],
                                                                  in1=gsS[:], op0=ALU.mult, op1=ALU.mult), [a, ev_prm])
            fr["ob"][i] = a
            fr["rl"][i] = a
            tcol = (qb % 4) * QB + qs * 128
            t = P.op("pe", lambda g, i=i: g.transpose(psT[:, 0:128], oa[i][:], identS[:]), [a, ev_c, fr["psT"]])
            fr["oa"][i] = t
            cp = P.op("act", lambda g, ot=ot, tcol=tcol: g.activation(out=outT[ot][:, tcol:tcol + 128], in_=psT[:, 0:128],
                                                                      func=AF.Copy), [t, fr["outT"][ot]])
            fr["psT"] = cp
        if qb % 4 == 3:
            c0 = (qb // 4) * 1024
            fr["outT"][ot] = P.dma("sp", yd_dst(c0), outT[ot][:], st[ot], [cp])
            finals.append(fr["outT"][ot])
    return finals


def build_phase_t(lambda_init):
    nc = bass.Bass("TRN2", target_bir_lowering=False)
    P = Prog(nc)
    io = {}
    def din(name, shape, dt=F32):
        io[name] = nc.dram_tensor(name, list(shape), dt, kind="ExternalInput").ap()
    din("qT", [128, SEQ], BF16); din("kT", [128, SEQ], BF16); din("V", [SEQ, 128], BF16)
    din("lq", [128, 4, 64]); din("gsub", [128, 128]); din("ident", [128, 128])
    io["ydT"] = nc.dram_tensor("ydT", [128, SEQ], F32, kind="ExternalOutput").ap()
    finals = emit_phase_t(nc, P, io, lambda_init)
    P.wait("sp", finals)
    return nc


def phase_t_params(inputs, l):
    lq = np.stack([inputs["diff_lq1"][l], inputs["diff_lk1"][l], inputs["diff_lq2"][l], inputs["diff_lk2"][l]])
    return {"lq": np.ascontiguousarray(np.broadcast_to(lq[None], (128, 4, 64))).astype(np.float32),
            "gsub": np.ascontiguousarray(np.broadcast_to(inputs["diff_subln"][l][None, :], (128, 128))).astype(np.float32),
            "ident": np.eye(128, dtype=np.float32)}


TBC = 256
NTBC = TOK // TBC
GELU_K = 2.0 * math.sqrt(2.0 / math.pi)


def emit_phase_c(nc, P, io):
    ysT, sgs, ydT, sgd, yx, xT, wglu, bglu, wout, gpost, xnT = (io.get(k) for k in
        ("ysT", "sgs", "ydT", "sgd", "yx", "xT", "wglu", "bglu", "wout", "gpost", "xnT"))
    in_dt = io.get("c_in_dt", F32)
    ys_ld = io.get("ys_ld", lambda dst, c0, c1, ld, deps: P.dma(
        "sp", dst[:], ysT.rearrange("c p t -> p c t")[:, :, c0:c1], ld, deps))
    yd_ld = io.get("yd_ld", lambda dst, c0, c1, ld, deps: P.dma(
        "sp", dst[:], ydT.rearrange("c p t -> p c t")[:, :, c0:c1], ld, deps))
    S = lambda shape, dt, name: P.sb(shape, dt, name)
    wob = S([128, 16, D], BF16, "wob")
    wgb = S([128, 8, 1024], BF16, "wgb")
    stg = [S([128, 16, 128], F32, "stg%d" % i) for i in range(2)]
    bgS = S([128, 8], F32, "bgS")
    gpS = S([128, 16], F32, "gpS")
    ones = S([128, 128], BF16, "ones")
    ysb = S([128, 8, TBC], in_dt, "ysb")
    sgsb = S([128, 8, TBC], F32, "sgsb")
    ydb = S([128, 4, TBC], in_dt, "ydb")
    sgdb = S([128, 4, TBC], F32, "sgdb")
    xb = S([128, 16, TBC], F32, "xb")
    x2 = S([128, 8, TBC], F32, "x2")
    ge = S([128, 8, TBC], F32, "ge")
    geb = S([128, 8, TBC], BF16, "geb")
    mixT = S([128, 16, TBC], BF16, "mixT")
    sgt = [S([128, TBC], F32, "sgt%d" % i) for i in range(2)]
    p1 = [S([128, TBC], F32, "p1%d" % i) for i in range(2)]
    oT = S([128, 16, TBC], F32, "oT")
    sqb = [S([128, TBC], BF16, "sqb%d" % i) for i in range(2)]
    rt = S([128, TBC], F32, "rt")
    rstd = S([128, TBC], F32, "rstd")
    tmpf = [S([128, TBC], F32, "tmpf%d" % i) for i in range(2)]

    psG = [P.ps("psG0"), P.ps("psG1")]
    psP = [P.ps("psP0"), P.ps("psP1")]
    psQ = P.ps("psQ")

    dc_ = P.dsem()
    P.dma("sp", bgS[:], bglu[:], dc_)
    ev_c = P.dma("sp", gpS[:], gpost[:], dc_)
    ev_ones = P.op("dve", lambda g: g.memset(ones[:], 1.0))
    wsem = [P.dsem(), P.dsem()]
    cast_ev = [None, None]
    ev_w = None
    jobs = [("g", i) for i in range(8)] + [("o", i) for i in range(16)]
    for n, (kind, i) in enumerate(jobs):
        s = n % 2
        if kind == "g":
            e = P.dma("sp", stg[s][:, 0:8, :], wglu[i], wsem[s], [cast_ev[s]])
            cast_ev[s] = P.op("pool", lambda g, s=s, i=i: g.tensor_copy(out=wgb[:, :, i * 128:(i + 1) * 128],
                                                                        in_=stg[s][:, 0:8, :]), [e])
        else:
            e = P.dma("sp", stg[s][:], wout[i], wsem[s], [cast_ev[s]])
            cast_ev[s] = P.op("pool", lambda g, s=s, i=i: g.tensor_copy(out=wob[:, :, i * 128:(i + 1) * 128],
                                                                        in_=stg[s][:]), [e])
    ev_w = [cast_ev[0], cast_ev[1]]

    ld = P.dsem()
    st = P.dsem()
    finals = []
    fr = {k: None for k in ("ysb", "sgsb", "ydb", "sgdb", "xb", "mix_x", "x2", "ge", "geb", "mixT", "oT", "psQ",
                            "rstd")}
    frl = {"psG": [None, None], "psP": [None, None], "sgt": [None, None], "p1": [None, None], "sqb": [None, None],
           "tmpf": [None, None]}
    gi = 0
    pi = 0
    for tb in range(NTBC):
        c0, c1 = tb * TBC, (tb + 1) * TBC
        e_ys = ys_ld(ysb, c0, c1, ld, [fr["ysb"]])
        e_sgs = P.dma("sp", sgsb[:], sgs.rearrange("c p t -> p c t")[:, :, c0:c1], ld, [fr["sgsb"]])
        e_yd = yd_ld(ydb, c0, c1, ld, [fr["ydb"]])
        e_sgd = P.dma("sp", sgdb[:], sgd.rearrange("c p t -> p c t")[:, :, c0:c1], ld, [fr["sgdb"]])
        e_yx = P.dma("sp", mixT[:, 12:16, :], yx.rearrange("c p t -> p c t")[:, :, c0:c1], ld, [fr["mixT"]])
        e_x = P.dma("sp", xb[:], xT[:, :, c0:c1], ld, [fr["xb"]])
        e_ld = e_x
        a = P.op("pool", lambda g: g.tensor_tensor(out=x2[:], in0=ysb[:], in1=ysb[:], op=ALU.mult), [e_ld, fr["x2"]])
        a = P.op("dve", lambda g: g.tensor_scalar(out=x2[:], in0=x2[:], scalar1=0.044715, scalar2=1.0, op0=ALU.mult,
                                                  op1=ALU.add), [a])
        a = P.op("pool", lambda g: g.tensor_tensor(out=x2[:], in0=x2[:], in1=ysb[:], op=ALU.mult), [a])
        a = P.op("act", lambda g: g.activation(out=x2[:], in_=x2[:], func=AF.Sigmoid, scale=GELU_K), [a])
        e_ge = P.op("dve", lambda g: g.tensor_tensor(out=ge[:], in0=ysb[:], in1=x2[:], op=ALU.mult), [a, fr["ge"]])
        fr["ysb"] = e_ge
        fr["x2"] = e_ge
        e_geb = P.op("pool", lambda g: g.tensor_copy(out=geb[:], in_=ge[:]), [e_ge, fr["geb"]])
        e_md = P.op("pool", lambda g: g.tensor_tensor(out=mixT[:, 8:12, :], in0=ydb[:], in1=sgdb[:], op=ALU.mult),
                    [e_ld, fr["mixT"]])
        fr["ydb"] = e_md
        fr["sgdb"] = e_md
        e_mix = None
        for co in range(8):
            g_ = gi % 2
            gi += 1
            for ci in range(8):
                e = P.op("pe", lambda g, g_=g_, ci=ci, co=co: g.matmul(psG[g_][:, 0:TBC],
                                                                       lhsT=wgb[:, ci, co * 128:(co + 1) * 128],
                                                                       rhs=geb[:, ci, :], start=(ci == 0), stop=(ci == 7)),
                         [e_geb, ev_w[0], ev_w[1]] + ([frl["psG"][g_]] if ci == 0 else []))
            e_sg = P.op("act", lambda g, g_=g_, co=co: g.activation(out=sgt[g_][:], in_=psG[g_][:, 0:TBC], func=AF.Sigmoid,
                                                                    bias=bgS[:, co:co + 1], scale=1.0),
                        [e, ev_c, frl["sgt"][g_]])
            frl["psG"][g_] = e_sg
            e_p = P.op("dve", lambda g, g_=g_, co=co: g.tensor_tensor(out=p1[g_][:], in0=ge[:, co, :], in1=sgt[g_][:],
                                                                      op=ALU.mult), [e_sg, frl["p1"][g_]])
            frl["sgt"][g_] = e_p
            e_mix = P.op("pool", lambda g, g_=g_, co=co: g.tensor_tensor(out=mixT[:, co, :], in0=p1[g_][:],
                                                                         in1=sgsb[:, co, :], op=ALU.mult),
                         [e_p, fr["mixT"]])
            frl["p1"][g_] = e_mix
        fr["ge"] = e_mix
        fr["geb"] = e
        fr["sgsb"] = e_mix
        e_sq_mm = None
        for dcn in range(16):
            p_ = pi % 2
            pi += 1
            for kc in range(16):
                e = P.op("pe", lambda g, p_=p_, kc=kc, dcn=dcn: g.matmul(psP[p_][:, 0:TBC],
                                                                         lhsT=wob[:, kc, dcn * 128:(dcn + 1) * 128],
                                                                         rhs=mixT[:, kc, :], start=(kc == 0),
                                                                         stop=(kc == 15)),
                         [e_mix, e_md, e_yx, ev_w[0], ev_w[1]] + ([frl["psP"][p_]] if kc == 0 else []))
            e_o = P.op("act", lambda g, p_=p_, dcn=dcn: g.activation(out=oT[:, dcn, :], in_=psP[p_][:, 0:TBC],
                                                                     func=AF.Copy), [e, fr["oT"]])
            e_s = P.op("act", lambda g, p_=p_: g.activation(out=sqb[p_][:], in_=psP[p_][:, 0:TBC], func=AF.Square),
                       [e, frl["sqb"][p_]])
            frl["psP"][p_] = e_s
            e_sq_mm = P.op("pe", lambda g, p_=p_, dcn=dcn: g.matmul(psQ[:, 0:TBC], lhsT=ones[:], rhs=sqb[p_][:],
                                                                    start=(dcn == 0), stop=(dcn == 15)),
                           [e_s, ev_ones] + ([fr["psQ"]] if dcn == 0 else []))
            frl["sqb"][p_] = e_sq_mm
        fr["mixT"] = e
        a = P.op("act", lambda g: g.activation(out=rt[:], in_=psQ[:, 0:TBC], func=AF.Sqrt, bias=EPS, scale=1.0 / D),
                 [e_sq_mm, fr["rstd"]])
        fr["psQ"] = a
        e_r = P.op("dve", lambda g: g.reciprocal(out=rstd[:], in_=rt[:]), [a])
        e_fin = None
        for dcn in range(16):
            t_ = dcn % 2
            a = P.op("dve", lambda g, t_=t_, dcn=dcn: g.scalar_tensor_tensor(out=tmpf[t_][:], in0=oT[:, dcn, :],
                                                                             scalar=gpS[:, dcn:dcn + 1], in1=rstd[:],
                                                                             op0=ALU.mult, op1=ALU.mult),
                     [e_r, e_o, ev_c, frl["tmpf"][t_]])
            e_fin = P.op("pool", lambda g, t_=t_, dcn=dcn: g.tensor_tensor(out=xb[:, dcn, :], in0=xb[:, dcn, :],
                                                                           in1=tmpf[t_][:], op=ALU.add), [a, e_x])
            frl["tmpf"][t_] = e_fin
        fr["oT"] = a
        fr["rstd"] = a
        fr["xb"] = P.dma("sp", xnT[:, :, c0:c1], xb[:], st, [e_fin])
        finals.append(fr["xb"])
    return finals


def build_phase_c():
    nc = bass.Bass("TRN2", target_bir_lowering=False)
    P = Prog(nc)
    io = {}
    def din(name, shape, dt=F32):
        io[name] = nc.dram_tensor(name, list(shape), dt, kind="ExternalInput").ap()
    din("ysT", [8, 128, TOK]); din("sgs", [8, 128, TOK]); din("ydT", [4, 128, TOK]); din("sgd", [4, 128, TOK])
    din("yx", [4, 128, TOK], BF16); din("xT", [128, KC, TOK]); din("wglu", [8, 128, 8, 128]); din("bglu", [128, 8])
    din("wout", [16, 128, 16, 128]); din("gpost", [128, 16])
    io["xnT"] = nc.dram_tensor("xnT", [128, KC, TOK], F32, kind="ExternalOutput").ap()
    finals = emit_phase_c(nc, P, io)
    P.wait("sp", finals)
    return nc


DEPTH = 2
S_KEYS = ("lr", "li", "ldt", "BR", "BI", "CRI", "CIR", "Dm")


def build_fused():
    nc = bass.Bass("TRN2", target_bir_lowering=False)
    P = Prog(nc)
    ext = {}

    def din(name, shape, dt=F32):
        ext[name] = nc.dram_tensor(name, list(shape), dt, kind="ExternalInput").ap()
        return ext[name]

    def dint(name, shape, dt):
        return nc.dram_tensor(name, list(shape), dt, kind="Internal").ap()

    din("xT", [128, KC, TOK]); din("memT", [128, KC, 256]); din("posr", [128, TOK], I32)
    din("ropec", [128, 2]); din("perm", [128, 128]); din("ident", [128, 128]); din("ramp", [128, 128])
    din("bmask", [128, 8]); din("sg", [128, 1])
    for l in range(DEPTH):
        sfx = "_%d" % l
        din("wall" + sfx, [48, 128, KC, 128]); din("gpre" + sfx, [128, KC]); din("gmem" + sfx, [128, KC])
        for k in ("lr", "li", "ldt"):
            din(k + sfx, [128, 8])
        for k in ("BR", "BI", "CRI", "CIR", "Dm"):
            din(k + sfx, [128, 8, 16])
        din("lq" + sfx, [128, 4, 64]); din("gsub" + sfx, [128, 128])
        din("wglu" + sfx, [8, 128, 8, 128]); din("bglu" + sfx, [128, 8]); din("wout" + sfx, [16, 128, 16, 128])
        din("gpost" + sfx, [128, KC])
    out = nc.dram_tensor("out", [128, KC, TOK], F32, kind="ExternalOutput").ap()

    X1 = dint("X1", [20, 128, 2048], BF16)
    G1u = dint("G1u", [8 * 1024, 2048], BF16)
    G1qkv = dint("G1qkv", [4 * 4096, 2048], BF16)
    Lu = dint("Lu", [1024, 2048], BF16)
    Lqkv = dint("Lqkv", [3, 512, 2048], BF16)
    Lys = dint("Lys", [1024, 2048], BF16)
    Lyd = dint("Lyd", [4, 128, 2048], BF16)

    def g1dst(gid):
        if gid < 8:
            return G1u[gid * 1024:(gid + 1) * 1024, :]
        r0 = (gid - 8) * 1024
        return G1qkv[r0:r0 + 1024, :]

    X2 = dint("X2", [12, 128, 2048], BF16)
    G2s = dint("G2s", [8 * 1024, 2048], BF16)
    G2d = dint("G2d", [5 * 1024, 2048], BF16)
    sgs_d = dint("sgs_d", [8, 128, TOK], F32)
    sgd_d = dint("sgd_d", [4, 128, TOK], F32)
    yx_d = dint("yx_d", [4, 128, TOK], BF16)
    xmid = dint("xmid", [128, KC, TOK], F32)

    cc = {"sem": nc.alloc_semaphore("cc"), "n": 0}
    RG = [list(range(NCORES))]

    def allgather(src2d, dst2d, deps):
        P.wait("pool", deps)
        nc.gpsimd.collective_compute("AllGather", ALU.bypass, replica_groups=RG, ins=[src2d.bitcast(F32)],
                                     outs=[dst2d.bitcast(F32)]).then_inc(cc["sem"], 1)
        cc["n"] += 1
        ev = (cc["sem"], cc["n"], "cc")
        P.wait("pool", [ev])
        return ev

    pid = nc.sync.partition_id()

    for l in range(DEPTH):
        sfx = "_%d" % l
        lambda_init = 0.8 - 0.6 * math.exp(-0.3 * l)
        x_src = ext["xT"] if l == 0 else xmid
        x_dst = xmid if l < DEPTH - 1 else out
        P.begin_phase("A%d" % l)
        q_cc = []
        itc = [0]
        ev_cc = [None]

        def on_store(gid, ev):
            q_cc.append((itc[0], gid, ev))

        def tick():
            itc[0] += 1

        ioA = {"xT": x_src, "wall": ext["wall" + sfx], "gpre": ext["gpre" + sfx], "gmem": ext["gmem" + sfx],
               "memT": ext["memT"], "posr": ext["posr"], "ropec": ext["ropec"], "perm": ext["perm"],
               "o_u": X1[0:8], "o_q": X1[8:12], "o_k": X1[12:16],
               "o_v": X1[16:20].rearrange("c p (t d) -> c p t d", d=128),
               "o_sgs": sgs_d, "o_sgd": sgd_d, "o_yx": yx_d, "on_store": on_store, "tick": tick}
        finA = emit_phase_a(nc, P, ioA)
        assert sorted(g for _, g, _ in q_cc) == list(range(20))
        P.end_phase(finA)
        for gid in range(20):
            ev_cc[0] = allgather(X1[gid], g1dst(gid), [])
        ev_x1 = ev_cc[0]
        P.begin_phase("S%d" % l)

        def u_loader(u, du):
            ev = None
            uv = u.rearrange("p b (q t) -> p (b q) t", t=2048)
            ev0 = P.dma("sp", Lu[:, :], G1u[bass.ds(pid * 1024, 1024), :], du, [ev_x1])
            for half in range(2):
                src = Lu[half * 512:(half + 1) * 512, :].rearrange("(s p) t -> p s t", p=128)
                ev = P.dma("sp", uv[:, half * 4:(half + 1) * 4, :], src, du, [ev0])
            return ev

        ioS = {k: ext[k + sfx] for k in S_KEYS}
        ioS.update({"sg": ext["sg"], "bmask": ext["bmask"], "ramp": ext["ramp"], "ident": ext["ident"],
                    "u_loader": u_loader, "ys_dt": BF16,
                    "ys_dst": lambda g_, b, t0: X2[b * 4 + t0 // 2048][g_ * 16:(g_ + 1) * 16,
                                                                      (t0 % 2048):(t0 % 2048) + SBLK]})
        finS = emit_phase_s(nc, P, ioS)
        P.end_phase(finS)
        ev_x2 = None
        for j in range(8):
            ev_x2 = allgather(X2[j], G2s[j * 1024:(j + 1) * 1024, :], [])
        P.begin_phase("T%d" % l)
        tbase = (pid // 2) * 1024 + (pid % 2) * 512
        dsel = P.dsem()
        ev_sel = P.dma("sp", Lqkv[:, :, :],
                       G1qkv[bass.ds(tbase, 3 * 4096), :].rearrange("(k r) t -> k r t", r=4096)[:, 0:512, :],
                       dsel, [ev_x1])
        ioT = {"lq": ext["lq" + sfx], "gsub": ext["gsub" + sfx], "ident": ext["ident"], "yd_dt": BF16,
               "yd_dst": lambda c0: X2[8 + c0 // 2048][:, (c0 % 2048):(c0 % 2048) + 1024],
               "q_src": lambda i: Lqkv[0, i * 128:(i + 1) * 128, :],
               "k_src": lambda i: Lqkv[1, i * 128:(i + 1) * 128, :],
               "v_src": lambda i: Lqkv[2, i * 128:(i + 1) * 128, :].rearrange("p (t d) -> p t d", d=128),
               "ld_deps": [ev_sel]}
        finT = emit_phase_t(nc, P, ioT, lambda_init)
        P.end_phase(finT)
        for j in range(8, 12):
            ev_x2 = allgather(X2[j], G2d[(j - 8) * 1024:(j - 7) * 1024, :], [])
        P.begin_phase("C%d" % l)

        dsel2 = P.dsem()
        P.dma("sp", Lys[:, :], G2s[bass.ds(pid * 1024, 1024), :], dsel2, [ev_x2])
        cbase = (pid % 4) * 1024 + (pid // 4) * 128
        ev_sel2 = P.dma("sp", Lyd[:, :, :],
                        G2d[bass.ds(cbase, 1024), :].rearrange("(h r p) t -> h r p t", r=2, p=128)[:, 0, :, :],
                        dsel2)

        def ys_ld(dst, c0, c1, ld, deps):
            src = Lys.rearrange("(c p) t -> p c t", p=128)[:, :, c0:c1]
            return P.dma("sp", dst[:], src, ld, list(deps) + [ev_sel2])

        def yd_ld(dst, c0, c1, ld, deps):
            return P.dma("sp", dst[:], Lyd.rearrange("h p t -> p h t")[:, :, c0:c1], ld, list(deps) + [ev_sel2])

        ioC = {"sgs": sgs_d, "sgd": sgd_d, "yx": yx_d, "xT": x_src, "wglu": ext["wglu" + sfx],
               "bglu": ext["bglu" + sfx], "wout": ext["wout" + sfx], "gpost": ext["gpost" + sfx], "xnT": x_dst,
               "c_in_dt": BF16, "ys_ld": ys_ld, "yd_ld": yd_ld}
        finC = emit_phase_c(nc, P, ioC)
        P.end_phase(finC)
    return nc


def kernel(**inputs):
    inputs = {k: np.asarray(v) for k, v in inputs.items()}
    ropec, perm = rope_consts()
    maps = []
    per_layer_common = []
    for l in range(DEPTH):
        tp = phase_t_params(inputs, l)
        per_layer_common.append({
            "wall": np.concatenate([wchunks(inputs["w_in"][l]), wchunks(inputs["w_mem_kv"][l])], axis=0),
            "gpre": np.ascontiguousarray(inputs["norm_pre"][l].reshape(KC, 128).T),
            "gmem": np.ascontiguousarray(inputs["norm_mem"][l].reshape(KC, 128).T),
            "lq": tp["lq"], "gsub": tp["gsub"],
            "wglu": wchunks(inputs["w_glu"][l]), "wout": wchunks(inputs["w_out"][l]),
            "bglu": np.ascontiguousarray(inputs["b_glu"][l].reshape(8, 128).T),
            "gpost": np.ascontiguousarray(inputs["norm_post"][l].reshape(KC, 128).T)})
    for c in range(NCORES):
        b, tq = c // 4, c % 4
        xs = inputs["x"][b, tq * TOK:(tq + 1) * TOK, :]
        m = {"xT": chunkT(np.ascontiguousarray(xs.T), KC),
             "memT": chunkT(np.ascontiguousarray(inputs["mem"][b].T), KC),
             "posr": np.ascontiguousarray(np.broadcast_to(inputs["positions"][b, tq * TOK:(tq + 1) * TOK][None, :],
                                                          (128, TOK))).astype(np.int32),
             "ropec": ropec, "perm": perm}
        for l in range(DEPTH):
            sp = phase_s_params(inputs, l, c)
            for k in ("ident", "ramp", "bmask", "sg"):
                m[k] = sp[k]
            for k in S_KEYS:
                m["%s_%d" % (k, l)] = sp[k]
            for k, v in per_layer_common[l].items():
                m["%s_%d" % (k, l)] = v
        maps.append(m)
    res = run_bass_kernel_spmd(build_fused(), maps, core_ids=list(range(NCORES))).results
    out = np.empty((BATCH, SEQ, D), np.float32)
    for c in range(NCORES):
        b, tq = c // 4, c % 4
        out[b, tq * TOK:(tq + 1) * TOK, :] = np.asarray(res[c]["out"]).transpose(1, 0, 2).reshape(D, TOK).T
    return out


def _run(nc, maps):
    res = run_bass_kernel_spmd(nc, maps, core_ids=list(range(NCORES)))
    return res.results


def kernel_unfused(**inputs):
    inputs = {k: np.asarray(v) for k, v in inputs.items()}
    depth = inputs["w_in"].shape[0]
    xT_cur = None
    for l in range(depth):
        lambda_init = 0.8 - 0.6 * math.exp(-0.3 * l)
        if l == 0:
            mapsA = phase_a_inputs(inputs, l, inputs["x"])
        else:
            mapsA = phase_a_inputs(inputs, l, None, xT_list=xT_cur)
        rA = _run(build_phase_a(), mapsA)
        xT_list = [m["xT"] for m in mapsA]
        mapsS = []
        for c in range(NCORES):
            m = phase_s_params(inputs, l, c)
            uT = np.empty((128, 2, SEQ), dtype=rA[0]["o_u"].dtype)
            for src in range(NCORES):
                b, tq = src // 4, src % 4
                uT[:, b, tq * TOK:(tq + 1) * TOK] = rA[src]["o_u"][c]
            m["uT"] = uT
            mapsS.append(m)
        rS = _run(build_phase_s(), mapsS)
        mapsT = []
        for c in range(NCORES):
            b, h = c // 4, c % 4
            m = phase_t_params(inputs, l)
            m["qT"] = np.ascontiguousarray(np.concatenate([rA[b * 4 + tq]["o_q"][h] for tq in range(4)], axis=1))
            m["kT"] = np.ascontiguousarray(np.concatenate([rA[b * 4 + tq]["o_k"][h] for tq in range(4)], axis=1))
            m["V"] = np.ascontiguousarray(np.concatenate(
                [np.asarray(rA[b * 4 + tq]["o_v"][h]).transpose(1, 0, 2).reshape(TOK, 128) for tq in range(4)], axis=0))
            mapsT.append(m)
        rT = _run(build_phase_t(lambda_init), mapsT)
        wglu = wchunks(inputs["w_glu"][l])
        wout = wchunks(inputs["w_out"][l])
        bglu = np.ascontiguousarray(inputs["b_glu"][l].reshape(8, 128).T)
        gpost = np.ascontiguousarray(inputs["norm_post"][l].reshape(KC, 128).T)
        mapsC = []
        for c in range(NCORES):
            b, tq = c // 4, c % 4
            sl = slice(tq * TOK, (tq + 1) * TOK)
            ysT = np.ascontiguousarray(np.stack([rS[g]["ys"][:, b, sl] for g in range(8)]))
            ydT = np.ascontiguousarray(np.stack([rT[b * 4 + h]["ydT"][:, sl] for h in range(4)]))
            mapsC.append({"ysT": ysT, "sgs": rA[c]["o_sgs"], "ydT": ydT, "sgd": rA[c]["o_sgd"], "yx": rA[c]["o_yx"],
                          "xT": xT_list[c], "wglu": wglu, "bglu": bglu, "wout": wout, "gpost": gpost})
        rC = _run(build_phase_c(), mapsC)
        xT_cur = [np.asarray(rC[c]["xnT"]) for c in range(NCORES)]
    out = np.empty((BATCH, SEQ, D), np.float32)
    for c in range(NCORES):
        b, tq = c // 4, c % 4
        out[b, tq * TOK:(tq + 1) * TOK, :] = xT_cur[c].transpose(1, 0, 2).reshape(D, TOK).T
    return out
```

```python
import math
import numpy as np
import concourse.bass as bass
import concourse.mybir as mybir
from concourse.bass_utils import run_bass_kernel_spmd

F32 = mybir.dt.float32
BF16 = mybir.dt.bfloat16
I32 = mybir.dt.int32
AF = mybir.ActivationFunctionType
ALU = mybir.AluOpType

NCORES = 8
D = 2048
SEQ = 8192
BATCH = 2
TOK = 2048
TB = 512
NTB = TOK // TB
KC = D // 128
EPS = 1e-6
PI = math.pi


class Prog:
    def __init__(self, nc):
        self.nc = nc
        self.eng = {"pe": nc.tensor, "act": nc.scalar, "dve": nc.vector, "pool": nc.gpsimd,
                    "sp": nc.sync}
        self.sem = {e: nc.alloc_semaphore("s_" + e) for e in ("pe", "act", "dve", "pool")}
        self.cnt = {e: 0 for e in self.sem}
        self.seen = {}
        self.nsem = 0
        self.names = 0
        self.stack = None
        self.tag = ""

    def wait(self, e, deps):
        for ev in deps:
            if ev is None:
                continue
            sem, val, key = ev
            if self.seen.get((e, key), 0) >= val:
                continue
            self.eng[e].wait_ge(sem, val)
            self.seen[(e, key)] = val

    def op(self, e, fn, deps=()):
        self.wait(e, deps)
        ins = fn(self.eng[e])
        self.cnt[e] += 1
        ins.then_inc(self.sem[e], 1)
        return (self.sem[e], self.cnt[e], e)

    def dsem(self):
        self.nsem += 1
        return {"sem": self.nc.alloc_semaphore("d%d" % self.nsem), "val": 0, "key": "d%d" % self.nsem}

    def dma(self, q, out, in_, ds, deps=(), **kw):
        self.wait(q, deps)
        ins = self.eng[q].dma_start(out=out, in_=in_, **kw)
        ds["val"] += 16
        ins.then_inc(ds["sem"], 16)
        return (ds["sem"], ds["val"], ds["key"])

    def sb(self, shape, dt, name=None):
        self.names += 1
        nm = self.tag + (name or ("t%d" % self.names))
        if self.stack is not None:
            return self.stack.enter_context(self.nc.sbuf_tensor(nm, list(shape), dt)).ap()
        return self.nc.alloc_sbuf_tensor(nm, list(shape), dt).ap()

    def ps(self, name=None):
        self.names += 1
        nm = self.tag + (name or ("p%d" % self.names))
        if self.stack is not None:
            return self.stack.enter_context(self.nc.psum_tensor(nm, [128, 512], F32)).ap()
        return self.nc.alloc_psum_tensor(nm, [128, 512], F32).ap()

    def begin_phase(self, tag):
        from contextlib import ExitStack
        self.tag = tag
        self.stack = ExitStack()

    def barrier(self, extra=()):
        evs = [(self.sem[e], self.cnt[e], e) for e in self.sem if self.cnt[e] > 0] + list(extra)
        for e in ("pe", "act", "dve", "pool", "sp"):
            self.wait(e, evs)

    def end_phase(self, extra=()):
        self.barrier(extra)
        self.stack.close()
        self.stack = None
        self.tag = ""


def wrap_sin(P, e, out, y, tmp, deps=()):
    ev = P.op(e, lambda g: g.tensor_scalar(out=tmp, in0=y, scalar1=PI, scalar2=-2 * PI,
                                           op0=ALU.is_gt, op1=ALU.mult), deps)
    ev = P.op(e, lambda g: g.tensor_tensor(out=y, in0=y, in1=tmp, op=ALU.add), [ev])
    ev = P.op(e, lambda g: g.tensor_scalar(out=tmp, in0=y, scalar1=-PI, scalar2=2 * PI,
                                           op0=ALU.is_lt, op1=ALU.mult), [ev])
    ev = P.op(e, lambda g: g.tensor_tensor(out=y, in0=y, in1=tmp, op=ALU.add), [ev])
    ev = P.op(e, lambda g: g.tensor_scalar(out=y, in0=y, scalar1=PI, scalar2=-PI,
                                           op0=ALU.min, op1=ALU.max), [ev])
    ev = P.op("act", lambda g: g.activation(out=out, in_=y, func=AF.Sin), [ev])
    return ev


def _kind(ci):
    if ci >= 44:
        return "mv"
    if ci >= 40:
        return "mk"
    return ("u", "gs", "q", "k", "v", "gd", "qx", "gx")[(0, 0, 1, 1, 2, 3, 4, 5, 6, 7)[ci // 4]] \
        if False else (["u"] * 8 + ["gs"] * 8 + ["q"] * 4 + ["k"] * 4 + ["v"] * 4 + ["gd"] * 4
                       + ["qx"] * 4 + ["gx"] * 4)[ci]


A_ORDER = list(range(40, 48)) + list(range(0, 32)) + [36, 32, 37, 33, 38, 34, 39, 35]


def emit_phase_a(nc, P, io):
    xT, wall, gpre, gmem, memT, posr, ropec, perm = (io[k] for k in
        ("xT", "wall", "gpre", "gmem", "memT", "posr", "ropec", "perm"))
    o_u, o_sgs, o_q, o_k, o_v, o_sgd, o_yx = (io[k] for k in
        ("o_u", "o_sgs", "o_q", "o_k", "o_v", "o_sgd", "o_yx"))

    hT = P.sb([128, KC, TOK], BF16, "hT")
    xs = [P.sb([128, 4, TB], F32, "xs%d" % i) for i in range(2)]
    sq = [P.sb([128, 4, TB], BF16, "sq%d" % i) for i in range(2)]
    rstd = P.sb([128, TOK], F32, "rstd")
    rstm = P.sb([128, 256], F32, "rstm")
    rtmp = P.sb([128, TB], F32, "rtmp")
    wf = [P.sb([128, KC, 128], F32, "wf%d" % i) for i in range(2)]
    wb = [P.sb([128, KC, 128], BF16, "wb%d" % i) for i in range(2)]
    outF = [P.sb([128, TOK], F32, "outF%d" % i) for i in range(2)]
    outB = [P.sb([128, TOK], BF16, "outB%d" % i) for i in range(2)]
    sgx = P.sb([128, TOK], F32, "sgx")
    cosT = P.sb([128, TOK], F32, "cosT")
    sinT = P.sb([128, TOK], F32, "sinT")
    memn = P.sb([128, KC, 256], BF16, "memn")
    mkT = P.sb([128, 4, 256], BF16, "mkT")
    mv = P.sb([128, 2, 512], BF16, "mv")
    ones = P.sb([128, 128], BF16, "ones")
    permS = P.sb([128, 128], F32, "permS")
    gpreS = P.sb([128, KC], F32, "gpreS")
    gmemS = P.sb([128, KC], F32, "gmemS")
    ropeS = P.sb([128, 2], F32, "ropeS")
    qf = [P.sb([128, TB], F32, "qf%d" % i) for i in range(2)]
    t1 = [P.sb([128, TB], F32, "t1%d" % i) for i in range(2)]
    t2 = [P.sb([128, TB], F32, "t2%d" % i) for i in range(2)]
    qxb = [P.sb([128, TB], BF16, "qxb%d" % i) for i in range(2)]
    PT = [P.sb([128, 2, TB], BF16, "PT%d" % i) for i in range(2)]
    rc = [P.sb([128, TB], F32, "rc%d" % i) for i in range(2)]

    psA = [P.ps("psA0"), P.ps("psA1")]
    psSS = P.ps("psSS")
    psR = P.ps("psR")
    psS = [P.ps("psS0"), P.ps("psS1")]
    psO = P.ps("psO")
    psL = P.ps("psL")

    dconst = P.dsem()
    ev_c = None
    for dst, src in ((permS, perm), (gpreS, gpre), (gmemS, gmem), (ropeS, ropec)):
        ev_c = P.dma("sp", dst[:], src[:], dconst)
    posI = sgx.bitcast(I32)
    ev_pos = P.dma("sp", posI[:], posr[:], dconst)
    ev_c = ev_pos
    ev_ones = P.op("dve", lambda g: g.memset(ones[:], 1.0))

    tA, tB_ = outF[0], outF[1]
    e = P.op("dve", lambda g: g.tensor_copy(out=tA[:], in_=posI[:]), [ev_pos])
    e = P.op("dve", lambda g: g.tensor_scalar(out=tA[:], in0=tA[:], scalar1=ropeS[:, 0:1], scalar2=None,
                                              op0=ALU.mult), [e])
    uI = outB[0].bitcast(I32) if False else None
    iI = cosT.bitcast(I32)
    e = P.op("dve", lambda g: g.tensor_copy(out=iI[:], in_=tA[:]), [e])
    e = P.op("dve", lambda g: g.tensor_copy(out=tB_[:], in_=iI[:]), [e])
    e = P.op("dve", lambda g: g.tensor_tensor(out=tA[:], in0=tA[:], in1=tB_[:], op=ALU.subtract), [e])
    e_s = P.op("dve", lambda g: g.tensor_scalar(out=tB_[:], in0=tA[:], scalar1=2 * PI, scalar2=None,
                                                op0=ALU.mult), [e])
    e_c = P.op("dve", lambda g: g.tensor_scalar(out=tA[:], in0=tA[:], scalar1=2 * PI, scalar2=PI / 2,
                                                op0=ALU.mult, op1=ALU.add), [e_s])
    scr = sgx
    e1 = wrap_sin(P, "dve", cosT[:], tA[:], scr[:], [e_c])
    e2 = wrap_sin(P, "dve", sinT[:], tB_[:], scr[:], [e_s, e1])
    e2 = P.op("dve", lambda g: g.tensor_scalar(out=sinT[:], in0=sinT[:], scalar1=ropeS[:, 1:2], scalar2=None,
                                               op0=ALU.mult), [e2])
    ev_rope = e2
    ev_setup_bufs = e2

    ld_sem = [P.dsem(), P.dsem()]
    xs_free = [None, None]
    sq_free = [None, None]
    ss_free = [None]
    pieces = []
    for pas in range(2):
        for kg in range(4):
            pieces.append(("m", pas, 0, kg))
    for pas in range(2):
        for tb in range(NTB):
            for kg in range(4):
                pieces.append(("x", pas, tb, kg))
    ev_rstd = {}
    ev_h_last = {}
    for i, (wh, pas, tb, kg) in enumerate(pieces):
        b = i % 2
        if wh == "m":
            n = 256
            src = memT[:, kg * 4:(kg + 1) * 4, :]
            gS, rs, dst = gmemS, rstm[:, 0:256], memn
            c0 = 0
        else:
            n = TB
            src = xT[:, kg * 4:(kg + 1) * 4, tb * TB:(tb + 1) * TB]
            gS, rs, dst = gpreS, rstd[:, tb * TB:(tb + 1) * TB], hT
            c0 = tb * TB
        ev_ld = P.dma("sp", xs[b][:, :, 0:n], src, ld_sem[b], [xs_free[b]])
        if pas == 0:
            ev_sq = P.op("act", lambda g, b=b, n=n: g.activation(out=sq[b][:, :, 0:n], in_=xs[b][:, :, 0:n],
                                                                   func=AF.Square), [ev_ld, sq_free[b]])
            xs_free[b] = ev_sq
            for j in range(4):
                first = (kg == 0 and j == 0)
                last = (kg == 3 and j == 3)
                ev_mm = P.op("pe", lambda g, b=b, j=j, n=n, first=first, last=last:
                             g.matmul(psSS[:, 0:n], lhsT=ones[:], rhs=sq[b][:, j, 0:n], start=first, stop=last),
                             [ev_sq, ev_ones] + ([ss_free[0]] if first else []))
            sq_free[b] = ev_mm
            if kg == 3:
                e = P.op("act", lambda g, n=n: g.activation(out=rtmp[:, 0:n], in_=psSS[:, 0:n], func=AF.Sqrt,
                                                           bias=EPS, scale=1.0 / D), [ev_mm])
                ss_free[0] = e
                e = P.op("dve", lambda g, n=n, rs=rs: g.reciprocal(out=rs, in_=rtmp[:, 0:n]), [e])
                ev_rstd[(wh, tb)] = e
        else:
            for j in range(4):
                kc = kg * 4 + j
                e = P.op("dve", lambda g, b=b, j=j, n=n, kc=kc, gS=gS, rs=rs, dst=dst, c0=c0:
                         g.scalar_tensor_tensor(out=dst[:, kc, c0:c0 + n], in0=xs[b][:, j, 0:n],
                                                scalar=gS[:, kc:kc + 1], in1=rs, op0=ALU.mult, op1=ALU.mult),
                         [ev_ld, ev_rstd[(wh, tb)], ev_c])
            xs_free[b] = e
            ev_h_last[(wh, tb)] = e
    ev_memn = ev_h_last[("m", 0)]

    NCH = len(A_ORDER)
    w_sem = [P.dsem(), P.dsem()]
    st_sem = [P.dsem(), P.dsem()]
    ev_wld = [None] * NCH
    ev_cast = [None] * NCH
    ev_pe_done = [None] * NCH
    ob_free = [ev_setup_bufs, ev_setup_bufs]
    psA_free = [None, None]
    cnt = {"acc": 0, "rope": 0, "ma": 0}
    free_evs = {"qf": [None, None], "t1": [None, None], "t2": [None, None], "psR": None,
                "qxb": [None, None], "PT": [None, None], "psS": [None, None], "psO": None, "psL": None,
                "rc": [None, None]}
    pending = []
    finals = []

    def load_w(n):
        ev_wld[n] = P.dma("sp", wf[n % 2][:], wall[A_ORDER[n]], w_sem[n % 2],
                          [ev_cast[n - 2]] if n >= 2 else [])

    def cast_w(n):
        ev_cast[n] = P.op("pool", lambda g, n=n: g.tensor_copy(out=wb[n % 2][:], in_=wf[n % 2][:]),
                          [ev_wld[n]] + ([ev_pe_done[n - 2]] if n >= 2 else []))

    load_w(0)
    load_w(1)
    cast_w(0)
    SCALE_X = 128 ** -0.5
    on_store = io.get("on_store", lambda gid, ev: None)
    tick = io.get("tick", lambda: None)
    for n in range(NCH):
        ci = A_ORDER[n]
        kind = _kind(ci)
        tick()
        if n + 1 < NCH:
            cast_w(n + 1)
        if n + 2 < NCH:
            load_w(n + 2)
        wbn = wb[n % 2]
        ob = n % 2
        if kind == "mk":
            h = ci - 40
            a = cnt["acc"] % 2
            cnt["acc"] += 1
            for kc in range(KC):
                e = P.op("pe", lambda g, kc=kc, a=a: g.matmul(psA[a][:, 0:256], lhsT=wbn[:, kc, :], rhs=memn[:, kc, :],
                                                               start=(kc == 0), stop=(kc == KC - 1)),
                         [ev_cast[n], ev_memn] + ([psA_free[a]] if kc == 0 else []))
            ev_pe_done[n] = e
            psA_free[a] = P.op("act", lambda g, a=a, h=h: g.activation(out=mkT[:, h, :], in_=psA[a][:, 0:256],
                                                                        func=AF.Copy), [e])
            ev_mk = psA_free[a]
            continue
        if kind == "mv":
            h = ci - 44
            a = cnt["acc"] % 2
            cnt["acc"] += 1
            for mc in range(2):
                for kc in range(KC):
                    e = P.op("pe", lambda g, kc=kc, a=a, mc=mc:
                             g.matmul(psA[a][:, mc * 128:(mc + 1) * 128], lhsT=memn[:, kc, mc * 128:(mc + 1) * 128],
                                      rhs=wbn[:, kc, :], start=(kc == 0), stop=(kc == KC - 1)),
                             [ev_cast[n], ev_memn] + ([psA_free[a]] if (kc == 0 and mc == 0) else []))
            ev_pe_done[n] = e
            for mc in range(2):
                psA_free[a] = P.op("act", lambda g, a=a, h=h, mc=mc:
                                   g.activation(out=mv[:, mc, h * 128:(h + 1) * 128],
                                                in_=psA[a][:, mc * 128:(mc + 1) * 128], func=AF.Copy), [e])
            ev_mv = psA_free[a]
            continue
        if kind == "v":
            h = ci - 24
            vb = outB[ob].rearrange("p (t c) -> p t c", c=128)
            ev_last = None
            for tg in range(4):
                a = cnt["acc"] % 2
                cnt["acc"] += 1
                for j in range(4):
                    tt = tg * 4 + j
                    for kc in range(KC):
                        e = P.op("pe", lambda g, kc=kc, a=a, j=j, tt=tt:
                                 g.matmul(psA[a][:, j * 128:(j + 1) * 128], lhsT=hT[:, kc, tt * 128:(tt + 1) * 128],
                                          rhs=wbn[:, kc, :], start=(kc == 0), stop=(kc == KC - 1)),
                                 [ev_cast[n], ev_h_last[("x", tt // 4)]] +
                                 ([psA_free[a]] if (kc == 0 and j == 0) else []))
                for f in pending:
                    f()
                pending = []
                ev_last = P.op("act", lambda g, a=a, tg=tg:
                               g.activation(out=outB[ob][:, tg * 512:(tg + 1) * 512], in_=psA[a][:], func=AF.Copy),
                               [e, ob_free[ob]])
                psA_free[a] = ev_last
            ev_pe_done[n] = e
            ob_free[ob] = P.dma("sp", o_v[h], vb, st_sem[ob], [ev_last])
            finals.append(ob_free[ob])
            on_store(16 + h, ob_free[ob])
            continue
        ev_out_last = None
        for tb in range(NTB):
            a = cnt["acc"] % 2
            cnt["acc"] += 1
            c0, c1 = tb * TB, (tb + 1) * TB
            for kc in range(KC):
                e = P.op("pe", lambda g, kc=kc, a=a, c0=c0, c1=c1:
                         g.matmul(psA[a][:], lhsT=wbn[:, kc, :], rhs=hT[:, kc, c0:c1],
                                  start=(kc == 0), stop=(kc == KC - 1)),
                         [ev_cast[n], ev_h_last[("x", tb)]] + ([psA_free[a]] if kc == 0 else []))
            ev_mm = e
            if tb == NTB - 1:
                ev_pe_done[n] = e
            for f in pending:
                f()
            pending = []
            if kind == "u":
                ev = P.op("act", lambda g, a=a, c0=c0, c1=c1: g.activation(out=outB[ob][:, c0:c1], in_=psA[a][:],
                                                                           func=AF.Copy), [ev_mm, ob_free[ob]])
                psA_free[a] = ev
                ev_out_last = ev
            elif kind in ("gs", "gd", "gx"):
                dst = sgx if kind == "gx" else outF[ob]
                deps = [ev_mm] + ([ob_free[ob]] if kind != "gx" else [ev_setup_bufs, free_evs.get("sgx")])
                ev = P.op("act", lambda g, a=a, c0=c0, c1=c1, dst=dst: g.activation(out=dst[:, c0:c1], in_=psA[a][:],
                                                                                    func=AF.Silu), deps)
                psA_free[a] = ev
                ev_out_last = ev
                if kind == "gx":
                    ev_sgx = ev
            elif kind in ("q", "k"):
                r = cnt["rope"] % 2
                cnt["rope"] += 1
                ev = P.op("act", lambda g, a=a, r=r: g.activation(out=qf[r][:], in_=psA[a][:], func=AF.Copy),
                          [ev_mm, free_evs["qf"][r]])
                psA_free[a] = ev

                def rope_tail(r=r, ev=ev, c0=c0, c1=c1, ob=ob, last=(tb == NTB - 1), ci=ci, kind=kind):
                    nonlocal ev_out_last
                    e_p = P.op("pe", lambda g: g.matmul(psR[:], lhsT=permS[:], rhs=qf[r][:], start=True, stop=True),
                               [ev, ev_c, free_evs["psR"]])
                    e_1 = P.op("pool", lambda g: g.tensor_tensor(out=t1[r][:], in0=qf[r][:], in1=cosT[:, c0:c1],
                                                                 op=ALU.mult), [ev, ev_rope, free_evs["t1"][r]])
                    e_2 = P.op("dve", lambda g: g.tensor_tensor(out=t2[r][:], in0=psR[:], in1=sinT[:, c0:c1],
                                                                op=ALU.mult), [e_p, ev_rope, free_evs["t2"][r]])
                    free_evs["psR"] = e_2
                    e_3 = P.op("pool", lambda g: g.tensor_tensor(out=outB[ob][:, c0:c1], in0=t1[r][:], in1=t2[r][:],
                                                                 op=ALU.add), [e_1, e_2, ob_free[ob]])
                    free_evs["qf"][r] = e_1 if True else None
                    free_evs["qf"][r] = e_3
                    free_evs["t1"][r] = e_3
                    free_evs["t2"][r] = e_3
                    if last:
                        dstd = (o_q if kind == "q" else o_k)[ci - (16 if kind == "q" else 20)]
                        ob_free[ob] = P.dma("sp", dstd, outB[ob][:], st_sem[ob], [e_3])
                        finals.append(ob_free[ob])
                        on_store((8 if kind == "q" else 12) + ci - (16 if kind == "q" else 20), ob_free[ob])
                pending.append(rope_tail)
            elif kind == "qx":
                h = ci - 32
                r = cnt["ma"] % 2
                cnt["ma"] += 1
                ev = P.op("act", lambda g, a=a, r=r: g.activation(out=qxb[r][:], in_=psA[a][:], func=AF.Copy),
                          [ev_mm, free_evs["qxb"][r]])
                psA_free[a] = ev

                def ma_tail(r=r, ev=ev, c0=c0, c1=c1, ob=ob, last=(tb == NTB - 1), h=h):
                    e_s = []
                    for mc in range(2):
                        e = P.op("pe", lambda g, mc=mc: g.matmul(psS[mc][:], lhsT=mkT[:, h, mc * 128:(mc + 1) * 128],
                                                                 rhs=qxb[r][:], start=True, stop=True),
                                 [ev, ev_mk, free_evs["psS"][mc]])
                        e_s.append(e)
                    free_evs["qxb"][r] = e_s[1]
                    e_x = []
                    for mc in range(2):
                        e = P.op("act", lambda g, mc=mc: g.activation(out=PT[r][:, mc, :], in_=psS[mc][:], func=AF.Exp,
                                                                      scale=SCALE_X),
                                 [e_s[mc], free_evs["PT"][r]])
                        free_evs["psS"][mc] = e
                        e_x.append(e)
                    for mc in range(2):
                        e_o = P.op("pe", lambda g, mc=mc: g.matmul(psO[:], lhsT=mv[:, mc, h * 128:(h + 1) * 128],
                                                                   rhs=PT[r][:, mc, :], start=(mc == 0), stop=(mc == 1)),
                                   [e_x[mc], ev_mv, free_evs["psO"]])
                    for mc in range(2):
                        e_l = P.op("pe", lambda g, mc=mc: g.matmul(psL[:], lhsT=ones[:], rhs=PT[r][:, mc, :],
                                                                   start=(mc == 0), stop=(mc == 1)),
                                   [e_x[mc], free_evs["psL"]])
                    free_evs["PT"][r] = e_l
                    e_r = P.op("dve", lambda g: g.reciprocal(out=rc[r][:], in_=psL[:]), [e_l, free_evs["rc"][r]])
                    free_evs["psL"] = e_r
                    e_m = P.op("dve", lambda g: g.tensor_tensor(out=rc[r][:], in0=psO[:], in1=rc[r][:], op=ALU.mult),
                               [e_o, e_r])
                    free_evs["psO"] = e_m
                    e_f = P.op("pool", lambda g: g.tensor_tensor(out=outB[ob][:, c0:c1], in0=rc[r][:], in1=sgx[:, c0:c1],
                                                                 op=ALU.mult), [e_m, ev_sgx, ob_free[ob]])
                    free_evs["rc"][r] = e_f
                    if last:
                        free_evs["sgx"] = e_f
                        ob_free[ob] = P.dma("sp", o_yx[h], outB[ob][:], st_sem[ob], [e_f])
                        finals.append(ob_free[ob])
                pending.append(ma_tail)
        if kind == "u":
            ob_free[ob] = P.dma("sp", o_u[ci], outB[ob][:], st_sem[ob], [ev_out_last])
            finals.append(ob_free[ob])
            on_store(ci, ob_free[ob])
        elif kind == "gs":
            ob_free[ob] = P.dma("sp", o_sgs[ci - 8], outF[ob][:], st_sem[ob], [ev_out_last])
            finals.append(ob_free[ob])
        elif kind == "gd":
            ob_free[ob] = P.dma("sp", o_sgd[ci - 28], outF[ob][:], st_sem[ob], [ev_out_last])
            finals.append(ob_free[ob])
    for f in pending:
        f()
    pending = []
    return finals


def build_phase_a():
    nc = bass.Bass("TRN2", target_bir_lowering=False)
    P = Prog(nc)
    io = {}
    def din(name, shape, dt=F32):
        io[name] = nc.dram_tensor(name, list(shape), dt, kind="ExternalInput").ap()
    def dout(name, shape, dt):
        io[name] = nc.dram_tensor(name, list(shape), dt, kind="ExternalOutput").ap()
    din("xT", [128, KC, TOK]); din("wall", [48, 128, KC, 128]); din("gpre", [128, KC]); din("gmem", [128, KC])
    din("memT", [128, KC, 256]); din("posr", [128, TOK], I32); din("ropec", [128, 2]); din("perm", [128, 128])
    dout("o_u", [8, 128, TOK], BF16); dout("o_sgs", [8, 128, TOK], F32)
    dout("o_q", [4, 128, TOK], BF16); dout("o_k", [4, 128, TOK], BF16)
    dout("o_v", [4, 128, 16, 128], BF16); dout("o_sgd", [4, 128, TOK], F32); dout("o_yx", [4, 128, TOK], BF16)
    finals = emit_phase_a(nc, P, io)
    P.wait("sp", finals)
    return nc


def rope_consts():
    p = np.arange(128)
    i = (p % 64) % 32
    inv = (10000.0 ** (-(2.0 * i) / 64.0)).astype(np.float32)
    c = np.zeros((128, 2), np.float32)
    c[:, 0] = (inv.astype(np.float64) / (2 * np.pi)).astype(np.float32)
    c[:, 1] = np.where((p % 64) < 32, -1.0, 1.0)
    perm = np.zeros((128, 128), np.float32)
    partner = np.where((p % 64) < 32, p + 32, p - 32)
    perm[partner, p] = 1.0
    return c, perm


def chunkT(a, nrow_chunks):
    R, C = a.shape
    return np.ascontiguousarray(a.reshape(R // 128, 128, C).transpose(1, 0, 2))


def wchunks(w):
    R, C = w.shape
    return np.ascontiguousarray(w.reshape(R // 128, 128, C // 128, 128).transpose(2, 1, 0, 3))


def phase_a_inputs(inputs, l, x_cur, xT_list=None):
    wall = np.concatenate([wchunks(inputs["w_in"][l]), wchunks(inputs["w_mem_kv"][l])], axis=0)
    gpre = np.ascontiguousarray(inputs["norm_pre"][l].reshape(KC, 128).T)
    gmem = np.ascontiguousarray(inputs["norm_mem"][l].reshape(KC, 128).T)
    ropec, perm = rope_consts()
    maps = []
    for c in range(NCORES):
        b, tq = c // 4, c % 4
        if xT_list is not None:
            xT = xT_list[c]
        else:
            xs = x_cur[b, tq * TOK:(tq + 1) * TOK, :]
            xT = chunkT(np.ascontiguousarray(xs.T), KC)
        memT = chunkT(np.ascontiguousarray(inputs["mem"][b].T), KC)
        posr = np.ascontiguousarray(np.broadcast_to(inputs["positions"][b, tq * TOK:(tq + 1) * TOK][None, :],
                                                    (128, TOK))).astype(np.int32)
        maps.append({"xT": xT, "wall": wall, "gpre": gpre, "gmem": gmem, "memT": memT, "posr": posr,
                     "ropec": ropec, "perm": perm})
    return maps


SBLK = 1024


def emit_phase_s(nc, P, io):
    uT, ys = io.get("uT"), io.get("ys")
    ys_dt = io.get("ys_dt", F32)
    ys_dst = io.get("ys_dst", lambda g_, b, t0: ys[g_ * 16:(g_ + 1) * 16, b, t0:t0 + SBLK])
    lr, li, ldt, BR, BI, CRI, CIR, sg, Dm, bmask, ramp, ident = (io[k] for k in
        ("lr", "li", "ldt", "BR", "BI", "CRI", "CIR", "sg", "Dm", "bmask", "ramp", "ident"))
    S = lambda shape, dt, name: P.sb(shape, dt, name)
    u = S([128, 2, SEQ], BF16, "u")
    cosT = S([128, SEQ], F32, "cosT")
    nsinT = S([128, SEQ], F32, "nsinT")
    tt = S([128, 32, 64], F32, "tt")
    ti = S([128, 32, 64], I32, "ti")
    ab = S([128, 32, 64], F32, "ab")
    bt = [S([128, SBLK], F32, "bt%d" % i) for i in range(2)]
    w = [S([128, SBLK], F32, "w%d" % i) for i in range(2)]
    wc = [S([128, SBLK], BF16, "wc%d" % i) for i in range(2)]
    wsn = [S([128, SBLK], BF16, "wsn%d" % i) for i in range(2)]
    t1 = [S([128, 512], F32, "t1%d" % i) for i in range(2)]
    t2 = [S([128, 512], F32, "t2%d" % i) for i in range(2)]
    ysb = [S([16, SBLK], ys_dt, "ysb%d" % i) for i in range(2)]
    names = ("lr", "li", "ldt")
    prm = {k: S([128, 8], F32, "p_" + k) for k in names}
    tmp = {k: S([128, 8], F32, "q_" + k) for k in
           ("dt", "lrdt", "rmag", "f", "fr", "fa", "sn", "cs", "are", "aim", "nr", "den", "rden", "zre", "zim",
            "a", "b", "szim", "szre", "nzim", "f64", "F64")}
    tmpi = S([128, 8], I32, "tmpi")
    BRs, BIs, CRIs, CIRs, Dms = (S([128, 8, 16], F32, n) for n in ("BRs", "BIs", "CRIs", "CIRs", "Dms"))
    sgS = S([128, 1], F32, "sgS")
    nsgS = S([128, 1], F32, "nsgS")
    bmS = S([128, 8], F32, "bmS")
    rampS = S([128, 128], F32, "rampS")
    identS = S([128, 128], F32, "identS")
    Bc1 = S([128, 8, 16], F32, "Bc1")
    Bc2 = S([128, 8, 16], F32, "Bc2")
    btmp = S([128, 16], F32, "btmp")
    W1T = S([128, 128], F32, "W1T")
    W2T = S([128, 128], F32, "W2T")
    L1 = S([128, 8, 128], BF16, "L1")
    L2 = S([128, 8, 128], BF16, "L2")
    Cc = S([128, 8, 16], BF16, "Cc")
    Cs = S([128, 8, 16], BF16, "Cs")
    Db = S([128, 8, 16], BF16, "Db")
    gc = S([128, 8, 128], F32, "gc")
    gci = S([128, 8, 128], I32, "gci")
    hj = S([128, 8, 64], F32, "hj")
    hji = S([128, 8, 64], I32, "hji")

    psX1 = [P.ps("psX10"), P.ps("psX11")]
    psX2 = [P.ps("psX20"), P.ps("psX21")]
    psY = [P.ps("psY0"), P.ps("psY1")]
    psT = P.ps("psT")

    dc = P.dsem()
    ev = None
    for dst, src in ((prm["lr"], lr), (prm["li"], li), (prm["ldt"], ldt), (BRs, BR), (BIs, BI), (CRIs, CRI),
                     (CIRs, CIR), (sgS, sg), (Dms, Dm), (bmS, bmask), (rampS, ramp), (identS, ident)):
        ev = P.dma("sp", dst[:], src[:], dc)
    ev_c = ev
    du = P.dsem()
    ev_u = None
    if "u_loader" in io:
        ev_u = io["u_loader"](u, du)
    else:
        for b in range(2):
            for hq in range(2):
                ev_u = P.dma("sp", u[:, b, hq * 4096:(hq + 1) * 4096], uT[:, b, hq * 4096:(hq + 1) * 4096], du)

    T = tmp
    last = [ev_c]

    def V(fn, eng="dve"):
        last[0] = P.op(eng, fn, [last[0]])
        return last[0]

    def frac(dst, src, itmp):
        V(lambda g: g.tensor_copy(out=itmp, in_=src))
        V(lambda g: g.tensor_tensor(out=dst, in0=src, in1=itmp, op=ALU.subtract))

    TT = lambda o, a, b, op: V(lambda g: g.tensor_tensor(out=o, in0=a, in1=b, op=op))
    V(lambda g: g.activation(out=T["dt"][:], in_=prm["ldt"][:], func=AF.Exp), "act")
    TT(T["lrdt"][:], prm["lr"][:], T["dt"][:], ALU.mult)
    V(lambda g: g.activation(out=T["rmag"][:], in_=T["lrdt"][:], func=AF.Exp), "act")
    TT(T["f"][:], prm["li"][:], T["dt"][:], ALU.mult)
    V(lambda g: g.tensor_scalar(out=T["f"][:], in0=T["f"][:], scalar1=1.0 / (2 * PI), scalar2=None, op0=ALU.mult))
    frac(T["fr"][:], T["f"][:], tmpi[:])
    V(lambda g: g.activation(out=T["sn"][:], in_=T["fr"][:], func=AF.Sin, scale=2 * PI), "act")
    V(lambda g: g.activation(out=T["fa"][:], in_=T["fr"][:], func=AF.Abs), "act")
    V(lambda g: g.activation(out=T["cs"][:], in_=T["fa"][:], func=AF.Sin, scale=-2 * PI, bias=PI / 2), "act")
    TT(T["are"][:], T["rmag"][:], T["cs"][:], ALU.mult)
    TT(T["aim"][:], T["rmag"][:], T["sn"][:], ALU.mult)
    V(lambda g: g.tensor_scalar(out=T["nr"][:], in0=T["are"][:], scalar1=-1.0, scalar2=None, op0=ALU.add))
    TT(T["a"][:], prm["lr"][:], prm["lr"][:], ALU.mult)
    TT(T["b"][:], prm["li"][:], prm["li"][:], ALU.mult)
    TT(T["den"][:], T["a"][:], T["b"][:], ALU.add)
    V(lambda g: g.reciprocal(out=T["rden"][:], in_=T["den"][:]))
    TT(T["a"][:], T["nr"][:], prm["lr"][:], ALU.mult)
    TT(T["b"][:], T["aim"][:], prm["li"][:], ALU.mult)
    TT(T["zre"][:], T["a"][:], T["b"][:], ALU.add)
    TT(T["zre"][:], T["zre"][:], T["rden"][:], ALU.mult)
    TT(T["a"][:], T["aim"][:], prm["lr"][:], ALU.mult)
    TT(T["b"][:], T["nr"][:], prm["li"][:], ALU.mult)
    TT(T["zim"][:], T["a"][:], T["b"][:], ALU.subtract)
    TT(T["zim"][:], T["zim"][:], T["rden"][:], ALU.mult)
    V(lambda g: g.tensor_scalar(out=T["szim"][:], in0=T["zim"][:], scalar1=sgS[:, 0:1], scalar2=None, op0=ALU.mult))
    V(lambda g: g.tensor_scalar(out=T["szre"][:], in0=T["zre"][:], scalar1=sgS[:, 0:1], scalar2=None, op0=ALU.mult))
    V(lambda g: g.tensor_scalar(out=T["nzim"][:], in0=T["zim"][:], scalar1=-1.0, scalar2=None, op0=ALU.mult))
    V(lambda g: g.tensor_scalar(out=nsgS[:], in0=sgS[:], scalar1=-1.0, scalar2=None, op0=ALU.mult))
    for g_ in range(8):
        sl = slice(g_, g_ + 1)
        V(lambda g, g_=g_, sl=sl: g.tensor_scalar(out=btmp[:], in0=BIs[:, g_, :], scalar1=T["szim"][:, sl], scalar2=None,
                                                  op0=ALU.mult))
        V(lambda g, g_=g_, sl=sl: g.scalar_tensor_tensor(out=Bc1[:, g_, :], in0=BRs[:, g_, :], scalar=T["zre"][:, sl],
                                                         in1=btmp[:], op0=ALU.mult, op1=ALU.add))
        V(lambda g, g_=g_, sl=sl: g.tensor_scalar(out=btmp[:], in0=BRs[:, g_, :], scalar1=T["nzim"][:, sl], scalar2=None,
                                                  op0=ALU.mult))
        V(lambda g, g_=g_, sl=sl: g.scalar_tensor_tensor(out=Bc2[:, g_, :], in0=BIs[:, g_, :], scalar=T["szre"][:, sl],
                                                         in1=btmp[:], op0=ALU.mult, op1=ALU.add))
    for Bc, WT, L in ((Bc1, W1T, L1), (Bc2, W2T, L2)):
        V(lambda g, Bc=Bc: g.transpose(psT[:, 0:128], Bc.rearrange("p g h -> p (g h)"), identS[:]), "pe")
        V(lambda g, WT=WT: g.tensor_copy(out=WT[:], in_=psT[:, 0:128]))
        for g_ in range(8):
            V(lambda g, g_=g_, WT=WT, L=L: g.tensor_scalar(out=L[:, g_, :], in0=WT[:], scalar1=bmS[:, g_:g_ + 1],
                                                           scalar2=None, op0=ALU.mult))
    V(lambda g: g.tensor_scalar(out=Cc.rearrange("p g h -> p (g h)"), in0=CRIs.rearrange("p g h -> p (g h)"),
                                scalar1=nsgS[:, 0:1], scalar2=None, op0=ALU.mult))
    V(lambda g: g.tensor_copy(out=Cs.rearrange("p g h -> p (g h)"), in_=CIRs.rearrange("p g h -> p (g h)")))
    V(lambda g: g.tensor_copy(out=Db.rearrange("p g h -> p (g h)"), in_=Dms.rearrange("p g h -> p (g h)")))
    V(lambda g: g.tensor_scalar(out=T["f64"][:], in0=T["f"][:], scalar1=64.0, scalar2=None, op0=ALU.mult))
    frac(T["F64"][:], T["f64"][:], tmpi[:])
    for g_ in range(8):
        V(lambda g, g_=g_: g.tensor_scalar(out=gc[:, g_, :], in0=rampS[:, 0:128], scalar1=T["F64"][:, g_:g_ + 1],
                                           scalar2=None, op0=ALU.mult))
        V(lambda g, g_=g_: g.tensor_scalar(out=hj[:, g_, :], in0=rampS[:, 0:64], scalar1=T["f"][:, g_:g_ + 1],
                                           scalar2=None, op0=ALU.mult))
    frac(gc[:], gc[:], gci[:])
    frac(hj[:], hj[:], hji[:])
    ev_prm = last[0]

    st = [P.dsem(), P.dsem()]
    finals = []
    tab_free = None
    tmp_free = {"tt": None, "ti": None, "ab": None}
    fr = {"psX1": [None, None], "psX2": [None, None], "t1": [None, None], "t2": [None, None],
          "bt": [None, None], "w": [None, None], "wc": [None, None], "wsn": [None, None],
          "psY": [None, None], "ysb": [None, None]}
    pending = []
    it = 0
    sub = 0
    ev_scan_prev = None
    NB = SEQ // SBLK
    def stage2(i, b, tb, t0, g_, e_add, rbc):
        nonlocal ev_scan_prev, tab_free
        init = 0.0 if tb == 0 else w[1 - i][:, SBLK - 1:SBLK]
        e_sc = P.op("dve", lambda g: g.tensor_tensor_scan(
            out=w[i][:], data0=rbc, data1=bt[i][:], initial=init, op0=ALU.mult, op1=ALU.add),
            [e_add, fr["w"][i], ev_scan_prev])
        ev_scan_prev = e_sc
        fr["bt"][i] = e_sc
        ts = slice(t0, t0 + SBLK)
        e_wc = P.op("pool", lambda g: g.tensor_tensor(out=wc[i][:], in0=w[i][:], in1=cosT[:, ts], op=ALU.mult),
                    [e_sc, fr["wc"][i]])
        e_ws = P.op("pool", lambda g: g.tensor_tensor(out=wsn[i][:], in0=w[i][:], in1=nsinT[:, ts], op=ALU.mult),
                    [e_sc, fr["wsn"][i]])
        fr["w"][i] = e_ws
        tab_free = e_ws

        def ytail():
            ev_last = None
            for s in range(SBLK // 512):
                cs = slice(s * 512, (s + 1) * 512)
                us = slice(t0 + s * 512, t0 + (s + 1) * 512)
                y = (2 * i + s) % 2
                P.op("pe", lambda g: g.matmul(psY[y][0:16, :], lhsT=Cc[:, g_, :], rhs=wc[i][:, cs],
                                              start=True, stop=False), [e_wc, fr["psY"][y]])
                P.op("pe", lambda g: g.matmul(psY[y][0:16, :], lhsT=Cs[:, g_, :], rhs=wsn[i][:, cs],
                                              start=False, stop=False), [e_ws])
                e_y = P.op("pe", lambda g: g.matmul(psY[y][0:16, :], lhsT=Db[:, g_, :], rhs=u[:, b, us],
                                                    start=False, stop=True), [])
                ev_last = P.op("act", lambda g: g.activation(out=ysb[i][:, cs], in_=psY[y][0:16, :],
                                                             func=AF.Copy), [e_y, fr["ysb"][i]])
                fr["psY"][y] = ev_last
            fr["wc"][i] = e_y
            fr["wsn"][i] = e_y
            fr["ysb"][i] = P.dma("sp", ys_dst(g_, b, t0), ysb[i][:], st[i], [ev_last])
            finals.append(fr["ysb"][i])
        pending.append(ytail)

    pending2 = []
    for g_ in range(8):
        for f in pending2:
            f()
        pending2 = []
        for q in range(4):
            qs = slice(q * 2048, (q + 1) * 2048)
            e = P.op("pool", lambda g, g_=g_, q=q: g.tensor_tensor(
                out=tt[:], in0=gc[:, g_, q * 32:(q + 1) * 32].unsqueeze(2).to_broadcast([128, 32, 64]),
                in1=hj[:, g_, :].unsqueeze(1).to_broadcast([128, 32, 64]), op=ALU.add), [ev_prm, tmp_free["tt"]])
            e2 = P.op("dve", lambda g: g.tensor_copy(out=ti[:], in_=tt[:]), [e, tmp_free["ti"]])
            e3 = P.op("pool", lambda g: g.tensor_tensor(out=tt[:], in0=tt[:], in1=ti[:], op=ALU.subtract), [e2])
            tmp_free["ti"] = e3
            e4 = P.op("act", lambda g, qs=qs: g.activation(out=nsinT[:, qs], in_=tt.rearrange("p a b -> p (a b)"),
                                                           func=AF.Sin, scale=-2 * PI), [e3, tab_free])
            e5 = P.op("act", lambda g: g.activation(out=ab.rearrange("p a b -> p (a b)"),
                                                    in_=tt.rearrange("p a b -> p (a b)"), func=AF.Abs),
                      [e3, tmp_free["ab"]])
            tmp_free["tt"] = e5
            e6 = P.op("act", lambda g, qs=qs: g.activation(out=cosT[:, qs], in_=ab.rearrange("p a b -> p (a b)"),
                                                           func=AF.Sin, scale=-2 * PI, bias=PI / 2), [e5, tab_free])
            tmp_free["ab"] = e6
        ev_tab = e6
        rbc = T["rmag"][:, g_:g_ + 1].to_broadcast([128, SBLK])
        for b in range(2):
            for tb in range(NB):
                i = it % 2
                it += 1
                t0 = tb * SBLK
                e_add = None
                for s in range(SBLK // 512):
                    x = sub % 2
                    sub += 1
                    cs = slice(t0 + s * 512, t0 + (s + 1) * 512)
                    e1 = P.op("pe", lambda g, x=x, cs=cs: g.matmul(psX1[x][:], lhsT=L1[:, g_, :], rhs=u[:, b, cs],
                                                                   start=True, stop=True),
                              [ev_prm, ev_u, fr["psX1"][x]])
                    e2 = P.op("pe", lambda g, x=x, cs=cs: g.matmul(psX2[x][:], lhsT=L2[:, g_, :], rhs=u[:, b, cs],
                                                                   start=True, stop=True),
                              [ev_prm, ev_u, fr["psX2"][x]])
                    if s == 0:
                        for f in pending:
                            f()
                        pending = []
                    d1 = P.op("dve", lambda g, x=x, cs=cs: g.tensor_tensor(out=t1[x][:], in0=psX1[x][:], in1=cosT[:, cs],
                                                                           op=ALU.mult), [e1, ev_tab, fr["t1"][x]])
                    fr["psX1"][x] = d1
                    d2 = P.op("dve", lambda g, x=x, cs=cs: g.tensor_tensor(out=t2[x][:], in0=psX2[x][:], in1=nsinT[:, cs],
                                                                           op=ALU.mult), [e2, ev_tab, fr["t2"][x]])
                    fr["psX2"][x] = d2
                    e_add = P.op("pool", lambda g, x=x, s=s, i=i: g.tensor_tensor(
                        out=bt[i][:, s * 512:(s + 1) * 512], in0=t1[x][:], in1=t2[x][:], op=ALU.add),
                        [d1, d2, fr["bt"][i]])
                    fr["t1"][x] = e_add
                    fr["t2"][x] = e_add
                for f in pending2:
                    f()
                pending2 = []
                pending2.append(lambda i=i, b=b, tb=tb, t0=t0, g_=g_, e_add=e_add, rbc=rbc:
                                stage2(i, b, tb, t0, g_, e_add, rbc))
    for f in pending2:
        f()
    for f in pending:
        f()
    return finals


def build_phase_s():
    nc = bass.Bass("TRN2", target_bir_lowering=False)
    P = Prog(nc)
    io = {}
    def din(name, shape, dt=F32):
        io[name] = nc.dram_tensor(name, list(shape), dt, kind="ExternalInput").ap()
    din("uT", [128, 2, SEQ], BF16)
    for k in ("lr", "li", "ldt", "bmask"):
        din(k, [128, 8])
    for k in ("BR", "BI", "CRI", "CIR", "Dm"):
        din(k, [128, 8, 16])
    din("sg", [128, 1]); din("ramp", [128, 128]); din("ident", [128, 128])
    io["ys"] = nc.dram_tensor("ys", [128, 2, SEQ], F32, kind="ExternalOutput").ap()
    finals = emit_phase_s(nc, P, io)
    P.wait("sp", finals)
    return nc


def phase_s_params(inputs, l, c):
    gs = slice(8 * c, 8 * c + 8)
    a_re = inputs["ssm_a_re"][l, gs]
    a_im = inputs["ssm_a_im"][l, gs]
    dup = lambda m: np.ascontiguousarray(np.concatenate([m.T, m.T], axis=0)).astype(np.float32)
    b_re = inputs["ssm_b_re"][l, gs].transpose(1, 0, 2)
    b_im = inputs["ssm_b_im"][l, gs].transpose(1, 0, 2)
    c_re = inputs["ssm_c_re"][l, gs].transpose(2, 0, 1)
    c_im = inputs["ssm_c_im"][l, gs].transpose(2, 0, 1)
    cat = lambda a, b: np.ascontiguousarray(np.concatenate([a, b], axis=0)).astype(np.float32)
    d = inputs["ssm_d"][l, gs]
    Dm = np.zeros((128, 8, 16), np.float32)
    for g in range(8):
        Dm[16 * g + np.arange(16), g, np.arange(16)] = d[g]
    bmask = np.zeros((128, 8), np.float32)
    bmask[np.arange(128), np.arange(128) // 16] = 1.0
    sg = np.concatenate([-np.ones((64, 1)), np.ones((64, 1))]).astype(np.float32)
    return {"lr": dup(a_re), "li": dup(a_im),
            "ldt": np.ascontiguousarray(np.broadcast_to(inputs["ssm_log_dt"][l, gs][None, :], (128, 8))).astype(np.float32),
            "BR": cat(b_re, b_im), "BI": cat(b_im, b_re), "CRI": cat(c_re, c_im), "CIR": cat(c_im, c_re),
            "sg": sg, "Dm": Dm, "bmask": bmask,
            "ramp": np.ascontiguousarray(np.broadcast_to(np.arange(128, dtype=np.float32)[None, :], (128, 128))),
            "ident": np.eye(128, dtype=np.float32)}


QB = 256
NQB = SEQ // QB


def emit_phase_t(nc, P, io, lambda_init):
    qT, kT, Vd, lq, gsub, ident, ydT = (io.get(k) for k in ("qT", "kT", "V", "lq", "gsub", "ident", "ydT"))
    yd_dt = io.get("yd_dt", F32)
    yd_dst = io.get("yd_dst", lambda c0: ydT[:, c0:c0 + 1024])
    q_src = io.get("q_src", lambda i: qT[:, i * 2048:(i + 1) * 2048])
    k_src = io.get("k_src", lambda i: kT[:, i * 2048:(i + 1) * 2048])
    v_src = io.get("v_src", lambda i: Vd[i * 2048:(i + 1) * 2048, :].rearrange("(t p) c -> p t c", p=128))
    ld_deps = io.get("ld_deps", [])
    S = lambda shape, dt, name: P.sb(shape, dt, name)
    q = S([128, SEQ], BF16, "q")
    k = S([128, SEQ], BF16, "k")
    Va = S([128, SEQ // 128, 129], BF16, "Va")
    NPT = 6
    PT = [S([128, QB], BF16, "PT%d" % i) for i in range(NPT)]
    PM = [S([128, 128], BF16, "PM%d" % i) for i in range(4)]
    lqS = S([128, 4, 64], F32, "lqS")
    lprod = S([128, 2, 64], F32, "lprod")
    lsum = S([128, 2], F32, "lsum")
    lexp = S([128, 2], F32, "lexp")
    nlam = S([128, 1], F32, "nlam")
    gsS = S([128, 128], F32, "gsS")
    identS = S([128, 128], F32, "identS")
    rl = [S([128, 4], F32, "rl%d" % i) for i in range(2)]
    oa = [S([128, 128], F32, "oa%d" % i) for i in range(2)]
    ob = [S([128, 128], F32, "ob%d" % i) for i in range(2)]
    junk = S([128, 128], F32, "junk")
    ssq = [S([128, 2], F32, "ssq%d" % i) for i in range(2)]
    outT = [S([128, 1024], yd_dt, "outT%d" % i) for i in range(2)]

    psO = [[P.ps("psO%d%d" % (c, s)) for s in range(2)] for c in range(2)]
    psS = [P.ps("psS%d" % i) for i in range(3)]
    psT = P.ps("psT")

    dc = P.dsem()
    e = P.dma("sp", lqS[:], lq[:], dc)
    e = P.dma("sp", gsS[:], gsub[:], dc)
    e = P.dma("sp", identS[:], ident[:], dc)
    ev_c = e
    dq = P.dsem()
    ev_q = [None] * 4
    v_stage = io.get("v_stage", False)
    if v_stage:
        Vs = [S([128, 2048 + 64], BF16, "Vs%d" % i)[:, 0:2048] for i in range(2)]
        vs_free = [None, None]
    for i in range(4):
        sl = slice(i * 2048, (i + 1) * 2048)
        P.dma("sp", k[:, sl], k_src(i), dq, ld_deps)
        e_qk = P.dma("sp", q[:, sl], q_src(i), dq)
        if v_stage:
            e_v = P.dma("sp", Vs[i % 2][:], v_src(i), dq, [vs_free[i % 2]])
            e_cp = P.op("pool", lambda g, i=i: g.tensor_copy(out=Va[:, i * 16:(i + 1) * 16, 0:128],
                                                            in_=Vs[i % 2].rearrange("p (t d) -> p t d", d=128)), [e_v])
            vs_free[i % 2] = e_cp
            ev_q[i] = (e_v, e_cp)
        else:
            ev_q[i] = (P.dma("sp", Va[:, i * 16:(i + 1) * 16, 0:128], v_src(i), dq), None)
    ev_ones = P.op("pool", lambda g: g.memset(Va[:, :, 128:129], 1.0))
    e = P.op("dve", lambda g: g.tensor_tensor(out=lprod[:, 0, :], in0=lqS[:, 0, :], in1=lqS[:, 1, :], op=ALU.mult), [ev_c])
    e = P.op("dve", lambda g: g.tensor_tensor(out=lprod[:, 1, :], in0=lqS[:, 2, :], in1=lqS[:, 3, :], op=ALU.mult), [e])
    e = P.op("dve", lambda g: g.tensor_reduce(out=lsum[:], in_=lprod[:], axis=mybir.AxisListType.X, op=ALU.add), [e])
    e = P.op("act", lambda g: g.activation(out=lexp[:], in_=lsum[:], func=AF.Exp), [e])
    e = P.op("dve", lambda g: g.tensor_tensor(out=nlam[:], in0=lexp[:, 1:2], in1=lexp[:, 0:1], op=ALU.subtract), [e])
    e = P.op("dve", lambda g: g.tensor_scalar(out=nlam[:], in0=nlam[:], scalar1=-float(lambda_init), scalar2=None,
                                              op0=ALU.add), [e])
    e = P.op("dve", lambda g: g.tensor_scalar(out=gsS[:], in0=gsS[:], scalar1=float(1.0 - lambda_init), scalar2=None,
                                              op0=ALU.mult), [e])
    ev_prm = e

    SC = 64 ** -0.5
    fr = {"psS": [None] * 3, "PT": [None] * NPT, "PM": [None] * 4, "psO": [[None, None], [None, None]],
          "psT": None, "outT": [None, None], "oa": [None, None], "ob": [None, None], "rl": [None, None],
          "ssq": [None, None], "junk": None}
    cnt = {"s": 0, "p": 0, "m": 0, "e": 0}
    st = [P.dsem(), P.dsem()]
    finals = []
    pending = []
    for qb in range(NQB):
        q0 = qb * QB
        nkt = 2 * qb + 2
        evq = ev_q[min(3, (q0 + QB - 1) // 2048)]
        last_pv = [[None, None], [None, None]]
        for kt in range(nkt):
            o = kt - 2 * qb
            c_lo = 128 if o == 1 else 0
            ncol = QB - c_lo
            pts = []
            for c in range(2):
                si = cnt["s"] % 3
                cnt["s"] += 1
                pi = cnt["p"] % NPT
                cnt["p"] += 1
                rows = slice(c * 64, (c + 1) * 64)
                e_s = P.op("pe", lambda g, si=si, rows=rows, kt=kt, c_lo=c_lo, ncol=ncol:
                           g.matmul(psS[si][:, 0:ncol], lhsT=k[rows, kt * 128:(kt + 1) * 128],
                                    rhs=q[rows, q0 + c_lo:q0 + QB], start=True, stop=True),
                           list(evq) + [fr["psS"][si]])
                e_x = P.op("act", lambda g, si=si, pi=pi, c_lo=c_lo, ncol=ncol:
                           g.activation(out=PT[pi][:, c_lo:QB], in_=psS[si][:, 0:ncol], func=AF.Exp, scale=SC),
                           [e_s, fr["PT"][pi]])
                fr["psS"][si] = e_x
                srcs = {}
                if o >= 0:
                    mi = cnt["m"] % 4
                    cnt["m"] += 1
                    e_m = P.op("pool", lambda g, mi=mi, pi=pi, c_lo=c_lo:
                               g.affine_select(out=PM[mi][:], in_=PT[pi][:, c_lo:c_lo + 128], pattern=[[1, 128]],
                                               compare_op=ALU.is_ge, fill=0.0, base=0, channel_multiplier=-1),
                               [e_x, fr["PM"][mi]])
                    srcs[o] = (PM[mi][:], e_m, ("PM", mi))
                    if o == 0:
                        srcs[1] = (PT[pi][:, 128:256], e_x, ("PT", pi))
                else:
                    srcs[0] = (PT[pi][:, 0:128], e_x, ("PT", pi))
                    srcs[1] = (PT[pi][:, 128:256], e_x, ("PT", pi))
                pts.append((srcs, pi))
            if kt == 0:
                pass
            for f in pending:
                f()
            pending = []

            def pv(pts=pts, kt=kt, qb=qb, nkt=nkt, last_pv=last_pv):
                for c in range(2):
                    srcs, pi = pts[c]
                    e_l = None
                    for qs, (ap, e_src, key) in sorted(srcs.items()):
                        first = (kt == 0)
                        lastk = (kt == 2 * qb + qs)
                        e_l = P.op("pe", lambda g, ap=ap, c=c, qs=qs, first=first, lastk=lastk:
                                   g.matmul(psO[c][qs][:, 0:129], lhsT=ap, rhs=Va[:, kt, :], start=first, stop=lastk),
                                   [e_src, ev_ones] + ([fr["psO"][c][qs]] if first else []))
                        last_pv[c][qs] = e_l
                        if key[0] == "PM":
                            fr["PM"][key[1]] = e_l
                    fr["PT"][pi] = e_l
            pending.append(pv)
        for f in pending:
            f()
        pending = []
        ot = (qb // 4) % 2
        for qs in range(2):
            i = cnt["e"] % 2
            cnt["e"] += 1
            e0, e1 = last_pv[0][qs], last_pv[1][qs]
            a = P.op("dve", lambda g, i=i, qs=qs: g.reciprocal(out=rl[i][:, 0:1], in_=psO[0][qs][:, 128:129]),
                     [e0, fr["rl"][i]])
            a = P.op("dve", lambda g, i=i, qs=qs: g.reciprocal(out=rl[i][:, 1:2], in_=psO[1][qs][:, 128:129]), [e1, a])
            a = P.op("dve", lambda g, i=i: g.tensor_tensor(out=rl[i][:, 2:3], in0=rl[i][:, 1:2], in1=nlam[:], op=ALU.mult),
                     [a, ev_prm])
            a = P.op("dve", lambda g, i=i, qs=qs: g.tensor_scalar(out=oa[i][:], in0=psO[0][qs][:, 0:128],
                                                                  scalar1=rl[i][:, 0:1], scalar2=None, op0=ALU.mult),
                     [a, fr["oa"][i]])
            fr["psO"][0][qs] = a
            a = P.op("dve", lambda g, i=i, qs=qs: g.scalar_tensor_tensor(out=ob[i][:], in0=psO[1][qs][:, 0:128],
                                                                         scalar=rl[i][:, 2:3], in1=oa[i][:],
                                                                         op0=ALU.mult, op1=ALU.add), [a, fr["ob"][i]])
            fr["psO"][1][qs] = a
            s1 = P.op("act", lambda g, i=i: g.activation(out=junk[:], in_=ob[i][:], func=AF.Square,
                                                         accum_out=ssq[i][:, 0:1]), [a, fr["junk"], fr["ssq"][i]])
            fr["junk"] = s1
            s2 = P.op("act", lambda g, i=i: g.activation(out=ssq[i][:, 1:2], in_=ssq[i][:, 0:1], func=AF.Sqrt,
                                                         bias=EPS, scale=1.0 / 128), [s1])
            a = P.op("dve", lambda g, i=i: g.reciprocal(out=rl[i][:, 3:4], in_=ssq[i][:, 1:2]), [s2])
            fr["ssq"][i] = a
            a = P.op("dve", lambda g, i=i: g.scalar_tensor_tensor(out=oa[i][:], in0=ob[i][:], scalar=rl[i][:, 3:4],
                                                                  in1=gsS[:], op0=ALU.mult, op1=ALU.mult), [a, ev_prm])
            fr["ob"][i] = a
            fr["rl"][i] = a
            tcol = (qb % 4) * QB + qs * 128
            t = P.op("pe", lambda g, i=i: g.transpose(psT[:, 0:128], oa[i][:], identS[:]), [a, ev_c, fr["psT"]])
            fr["oa"][i] = t
            cp = P.op("act", lambda g, ot=ot, tcol=tcol: g.activation(out=outT[ot][:, tcol:tcol + 128], in_=psT[:, 0:128],
                                                                      func=AF.Copy), [t, fr["outT"][ot]])
            fr["psT"] = cp
        if qb % 4 == 3:
            c0 = (qb // 4) * 1024
            fr["outT"][ot] = P.dma("sp", yd_dst(c0), outT[ot][:], st[ot], [cp])
            finals.append(fr["outT"][ot])
    return finals


def build_phase_t(lambda_init):
    nc = bass.Bass("TRN2", target_bir_lowering=False)
    P = Prog(nc)
    io = {}
    def din(name, shape, dt=F32):
        io[name] = nc.dram_tensor(name, list(shape), dt, kind="ExternalInput").ap()
    din("qT", [128, SEQ], BF16); din("kT", [128, SEQ], BF16); din("V", [SEQ, 128], BF16)
    din("lq", [128, 4, 64]); din("gsub", [128, 128]); din("ident", [128, 128])
    io["ydT"] = nc.dram_tensor("ydT", [128, SEQ], F32, kind="ExternalOutput").ap()
    finals = emit_phase_t(nc, P, io, lambda_init)
    P.wait("sp", finals)
    return nc


def phase_t_params(inputs, l):
    lq = np.stack([inputs["diff_lq1"][l], inputs["diff_lk1"][l], inputs["diff_lq2"][l], inputs["diff_lk2"][l]])
    return {"lq": np.ascontiguousarray(np.broadcast_to(lq[None], (128, 4, 64))).astype(np.float32),
            "gsub": np.ascontiguousarray(np.broadcast_to(inputs["diff_subln"][l][None, :], (128, 128))).astype(np.float32),
            "ident": np.eye(128, dtype=np.float32)}


TBC = 256
NTBC = TOK // TBC
GELU_K = 2.0 * math.sqrt(2.0 / math.pi)


def emit_phase_c(nc, P, io):
    ysT, sgs, ydT, sgd, yx, xT, wglu, bglu, wout, gpost, xnT = (io.get(k) for k in
        ("ysT", "sgs", "ydT", "sgd", "yx", "xT", "wglu", "bglu", "wout", "gpost", "xnT"))
    in_dt = io.get("c_in_dt", F32)
    ys_ld = io.get("ys_ld", lambda dst, c0, c1, ld, deps: P.dma(
        "sp", dst[:], ysT.rearrange("c p t -> p c t")[:, :, c0:c1], ld, deps))
    yd_ld = io.get("yd_ld", lambda dst, c0, c1, ld, deps: P.dma(
        "sp", dst[:], ydT.rearrange("c p t -> p c t")[:, :, c0:c1], ld, deps))
    S = lambda shape, dt, name: P.sb(shape, dt, name)
    wob = S([128, 16, D], BF16, "wob")
    wgb = S([128, 8, 1024], BF16, "wgb")
    stg = [S([128, 16, 128], F32, "stg%d" % i) for i in range(2)]
    bgS = S([128, 8], F32, "bgS")
    gpS = S([128, 16], F32, "gpS")
    ones = S([128, 128], BF16, "ones")
    ysb = S([128, 8, TBC], in_dt, "ysb")
    sgsb = S([128, 8, TBC], F32, "sgsb")
    ydb = S([128, 4, TBC], in_dt, "ydb")
    sgdb = S([128, 4, TBC], F32, "sgdb")
    xb = S([128, 16, TBC], F32, "xb")
    x2 = S([128, 8, TBC], F32, "x2")
    ge = S([128, 8, TBC], F32, "ge")
    geb = S([128, 8, TBC], BF16, "geb")
    mixT = S([128, 16, TBC], BF16, "mixT")
    sgt = [S([128, TBC], F32, "sgt%d" % i) for i in range(2)]
    p1 = [S([128, TBC], F32, "p1%d" % i) for i in range(2)]
    oT = S([128, 16, TBC], F32, "oT")
    sqb = [S([128, TBC], BF16, "sqb%d" % i) for i in range(2)]
    rt = S([128, TBC], F32, "rt")
    rstd = S([128, TBC], F32, "rstd")
    tmpf = [S([128, TBC], F32, "tmpf%d" % i) for i in range(2)]

    psG = [P.ps("psG0"), P.ps("psG1")]
    psP = [P.ps("psP0"), P.ps("psP1")]
    psQ = P.ps("psQ")

    dc_ = P.dsem()
    P.dma("sp", bgS[:], bglu[:], dc_)
    ev_c = P.dma("sp", gpS[:], gpost[:], dc_)
    ev_ones = P.op("dve", lambda g: g.memset(ones[:], 1.0))
    wsem = [P.dsem(), P.dsem()]
    cast_ev = [None, None]
    ev_w = None
    jobs = [("g", i) for i in range(8)] + [("o", i) for i in range(16)]
    for n, (kind, i) in enumerate(jobs):
        s = n % 2
        if kind == "g":
            e = P.dma("sp", stg[s][:, 0:8, :], wglu[i], wsem[s], [cast_ev[s]])
            cast_ev[s] = P.op("pool", lambda g, s=s, i=i: g.tensor_copy(out=wgb[:, :, i * 128:(i + 1) * 128],
                                                                        in_=stg[s][:, 0:8, :]), [e])
        else:
            e = P.dma("sp", stg[s][:], wout[i], wsem[s], [cast_ev[s]])
            cast_ev[s] = P.op("pool", lambda g, s=s, i=i: g.tensor_copy(out=wob[:, :, i * 128:(i + 1) * 128],
                                                                        in_=stg[s][:]), [e])
    ev_w = [cast_ev[0], cast_ev[1]]

    ld = P.dsem()
    st = P.dsem()
    finals = []
    fr = {k: None for k in ("ysb", "sgsb", "ydb", "sgdb", "xb", "mix_x", "x2", "ge", "geb", "mixT", "oT", "psQ",
                            "rstd")}
    frl = {"psG": [None, None], "psP": [None, None], "sgt": [None, None], "p1": [None, None], "sqb": [None, None],
           "tmpf": [None, None]}
    gi = 0
    pi = 0
    for tb in range(NTBC):
        c0, c1 = tb * TBC, (tb + 1) * TBC
        e_ys = ys_ld(ysb, c0, c1, ld, [fr["ysb"]])
        e_sgs = P.dma("sp", sgsb[:], sgs.rearrange("c p t -> p c t")[:, :, c0:c1], ld, [fr["sgsb"]])
        e_yd = yd_ld(ydb, c0, c1, ld, [fr["ydb"]])
        e_sgd = P.dma("sp", sgdb[:], sgd.rearrange("c p t -> p c t")[:, :, c0:c1], ld, [fr["sgdb"]])
        e_yx = P.dma("sp", mixT[:, 12:16, :], yx.rearrange("c p t -> p c t")[:, :, c0:c1], ld, [fr["mixT"]])
        e_x = P.dma("sp", xb[:], xT[:, :, c0:c1], ld, [fr["xb"]])
        e_ld = e_x
        a = P.op("pool", lambda g: g.tensor_tensor(out=x2[:], in0=ysb[:], in1=ysb[:], op=ALU.mult), [e_ld, fr["x2"]])
        a = P.op("dve", lambda g: g.tensor_scalar(out=x2[:], in0=x2[:], scalar1=0.044715, scalar2=1.0, op0=ALU.mult,
                                                  op1=ALU.add), [a])
        a = P.op("pool", lambda g: g.tensor_tensor(out=x2[:], in0=x2[:], in1=ysb[:], op=ALU.mult), [a])
        a = P.op("act", lambda g: g.activation(out=x2[:], in_=x2[:], func=AF.Sigmoid, scale=GELU_K), [a])
        e_ge = P.op("dve", lambda g: g.tensor_tensor(out=ge[:], in0=ysb[:], in1=x2[:], op=ALU.mult), [a, fr["ge"]])
        fr["ysb"] = e_ge
        fr["x2"] = e_ge
        e_geb = P.op("pool", lambda g: g.tensor_copy(out=geb[:], in_=ge[:]), [e_ge, fr["geb"]])
        e_md = P.op("pool", lambda g: g.tensor_tensor(out=mixT[:, 8:12, :], in0=ydb[:], in1=sgdb[:], op=ALU.mult),
                    [e_ld, fr["mixT"]])
        fr["ydb"] = e_md
        fr["sgdb"] = e_md
        e_mix = None
        for co in range(8):
            g_ = gi % 2
            gi += 1
            for ci in range(8):
                e = P.op("pe", lambda g, g_=g_, ci=ci, co=co: g.matmul(psG[g_][:, 0:TBC],
                                                                       lhsT=wgb[:, ci, co * 128:(co + 1) * 128],
                                                                       rhs=geb[:, ci, :], start=(ci == 0), stop=(ci == 7)),
                         [e_geb, ev_w[0], ev_w[1]] + ([frl["psG"][g_]] if ci == 0 else []))
            e_sg = P.op("act", lambda g, g_=g_, co=co: g.activation(out=sgt[g_][:], in_=psG[g_][:, 0:TBC], func=AF.Sigmoid,
                                                                    bias=bgS[:, co:co + 1], scale=1.0),
                        [e, ev_c, frl["sgt"][g_]])
            frl["psG"][g_] = e_sg
            e_p = P.op("dve", lambda g, g_=g_, co=co: g.tensor_tensor(out=p1[g_][:], in0=ge[:, co, :], in1=sgt[g_][:],
                                                                      op=ALU.mult), [e_sg, frl["p1"][g_]])
            frl["sgt"][g_] = e_p
            e_mix = P.op("pool", lambda g, g_=g_, co=co: g.tensor_tensor(out=mixT[:, co, :], in0=p1[g_][:],
                                                                         in1=sgsb[:, co, :], op=ALU.mult),
                         [e_p, fr["mixT"]])
            frl["p1"][g_] = e_mix
        fr["ge"] = e_mix
        fr["geb"] = e
        fr["sgsb"] = e_mix
        e_sq_mm = None
        for dcn in range(16):
            p_ = pi % 2
            pi += 1
            for kc in range(16):
                e = P.op("pe", lambda g, p_=p_, kc=kc, dcn=dcn: g.matmul(psP[p_][:, 0:TBC],
                                                                         lhsT=wob[:, kc, dcn * 128:(dcn + 1) * 128],
                                                                         rhs=mixT[:, kc, :], start=(kc == 0),
                                                                         stop=(kc == 15)),
                         [e_mix, e_md, e_yx, ev_w[0], ev_w[1]] + ([frl["psP"][p_]] if kc == 0 else []))
            e_o = P.op("act", lambda g, p_=p_, dcn=dcn: g.activation(out=oT[:, dcn, :], in_=psP[p_][:, 0:TBC],
                                                                     func=AF.Copy), [e, fr["oT"]])
            e_s = P.op("act", lambda g, p_=p_: g.activation(out=sqb[p_][:], in_=psP[p_][:, 0:TBC], func=AF.Square),
                       [e, frl["sqb"][p_]])
            frl["psP"][p_] = e_s
            e_sq_mm = P.op("pe", lambda g, p_=p_, dcn=dcn: g.matmul(psQ[:, 0:TBC], lhsT=ones[:], rhs=sqb[p_][:],
                                                                    start=(dcn == 0), stop=(dcn == 15)),
                           [e_s, ev_ones] + ([fr["psQ"]] if dcn == 0 else []))
            frl["sqb"][p_] = e_sq_mm
        fr["mixT"] = e
        a = P.op("act", lambda g: g.activation(out=rt[:], in_=psQ[:, 0:TBC], func=AF.Sqrt, bias=EPS, scale=1.0 / D),
                 [e_sq_mm, fr["rstd"]])
        fr["psQ"] = a
        e_r = P.op("dve", lambda g: g.reciprocal(out=rstd[:], in_=rt[:]), [a])
        e_fin = None
        for dcn in range(16):
            t_ = dcn % 2
            a = P.op("dve", lambda g, t_=t_, dcn=dcn: g.scalar_tensor_tensor(out=tmpf[t_][:], in0=oT[:, dcn, :],
                                                                             scalar=gpS[:, dcn:dcn + 1], in1=rstd[:],
                                                                             op0=ALU.mult, op1=ALU.mult),
                     [e_r, e_o, ev_c, frl["tmpf"][t_]])
            e_fin = P.op("pool", lambda g, t_=t_, dcn=dcn: g.tensor_tensor(out=xb[:, dcn, :], in0=xb[:, dcn, :],
                                                                           in1=tmpf[t_][:], op=ALU.add), [a, e_x])
            frl["tmpf"][t_] = e_fin
        fr["oT"] = a
        fr["rstd"] = a
        fr["xb"] = P.dma("sp", xnT[:, :, c0:c1], xb[:], st, [e_fin])
        finals.append(fr["xb"])
    return finals


def build_phase_c():
    nc = bass.Bass("TRN2", target_bir_lowering=False)
    P = Prog(nc)
    io = {}
    def din(name, shape, dt=F32):
        io[name] = nc.dram_tensor(name, list(shape), dt, kind="ExternalInput").ap()
    din("ysT", [8, 128, TOK]); din("sgs", [8, 128, TOK]); din("ydT", [4, 128, TOK]); din("sgd", [4, 128, TOK])
    din("yx", [4, 128, TOK], BF16); din("xT", [128, KC, TOK]); din("wglu", [8, 128, 8, 128]); din("bglu", [128, 8])
    din("wout", [16, 128, 16, 128]); din("gpost", [128, 16])
    io["xnT"] = nc.dram_tensor("xnT", [128, KC, TOK], F32, kind="ExternalOutput").ap()
    finals = emit_phase_c(nc, P, io)
    P.wait("sp", finals)
    return nc


DEPTH = 2
S_KEYS = ("lr", "li", "ldt", "BR", "BI", "CRI", "CIR", "Dm")


def build_fused():
    nc = bass.Bass("TRN2", target_bir_lowering=False)
    P = Prog(nc)
    ext = {}

    def din(name, shape, dt=F32):
        ext[name] = nc.dram_tensor(name, list(shape), dt, kind="ExternalInput").ap()
        return ext[name]

    def dint(name, shape, dt):
        return nc.dram_tensor(name, list(shape), dt, kind="Internal").ap()

    din("xT", [128, KC, TOK]); din("memT", [128, KC, 256]); din("posr", [128, TOK], I32)
    din("ropec", [128, 2]); din("perm", [128, 128]); din("ident", [128, 128]); din("ramp", [128, 128])
    din("bmask", [128, 8]); din("sg", [128, 1])
    for l in range(DEPTH):
        sfx = "_%d" % l
        din("wall" + sfx, [48, 128, KC, 128]); din("gpre" + sfx, [128, KC]); din("gmem" + sfx, [128, KC])
        for k in ("lr", "li", "ldt"):
            din(k + sfx, [128, 8])
        for k in ("BR", "BI", "CRI", "CIR", "Dm"):
            din(k + sfx, [128, 8, 16])
        din("lq" + sfx, [128, 4, 64]); din("gsub" + sfx, [128, 128])
        din("wglu" + sfx, [8, 128, 8, 128]); din("bglu" + sfx, [128, 8]); din("wout" + sfx, [16, 128, 16, 128])
        din("gpost" + sfx, [128, KC])
    out = nc.dram_tensor("out", [128, KC, TOK], F32, kind="ExternalOutput").ap()

    X1 = dint("X1", [20, 128, 2048], BF16)
    G1u = dint("G1u", [8 * 1024, 2048], BF16)
    G1qkv = dint("G1qkv", [4 * 4096, 2048], BF16)
    Lu = dint("Lu", [1024, 2048], BF16)
    Lqkv = dint("Lqkv", [3, 512, 2048], BF16)
    Lys = dint("Lys", [1024, 2048], BF16)
    Lyd = dint("Lyd", [4, 128, 2048], BF16)

    def g1dst(gid):
        if gid < 8:
            return G1u[gid * 1024:(gid + 1) * 1024, :]
        r0 = (gid - 8) * 1024
        return G1qkv[r0:r0 + 1024, :]

    X2 = dint("X2", [12, 128, 2048], BF16)
    G2s = dint("G2s", [8 * 1024, 2048], BF16)
    G2d = dint("G2d", [5 * 1024, 2048], BF16)
    sgs_d = dint("sgs_d", [8, 128, TOK], F32)
    sgd_d = dint("sgd_d", [4, 128, TOK], F32)
    yx_d = dint("yx_d", [4, 128, TOK], BF16)
    xmid = dint("xmid", [128, KC, TOK], F32)

    cc = {"sem": nc.alloc_semaphore("cc"), "n": 0}
    RG = [list(range(NCORES))]

    def allgather(src2d, dst2d, deps):
        P.wait("pool", deps)
        nc.gpsimd.collective_compute("AllGather", ALU.bypass, replica_groups=RG, ins=[src2d.bitcast(F32)],
                                     outs=[dst2d.bitcast(F32)]).then_inc(cc["sem"], 1)
        cc["n"] += 1
        ev = (cc["sem"], cc["n"], "cc")
        P.wait("pool", [ev])
        return ev

    pid = nc.sync.partition_id()

    for l in range(DEPTH):
        sfx = "_%d" % l
        lambda_init = 0.8 - 0.6 * math.exp(-0.3 * l)
        x_src = ext["xT"] if l == 0 else xmid
        x_dst = xmid if l < DEPTH - 1 else out
        P.begin_phase("A%d" % l)
        q_cc = []
        itc = [0]
        ev_cc = [None]

        def on_store(gid, ev):
            q_cc.append((itc[0], gid, ev))

        def tick():
            itc[0] += 1

        ioA = {"xT": x_src, "wall": ext["wall" + sfx], "gpre": ext["gpre" + sfx], "gmem": ext["gmem" + sfx],
               "memT": ext["memT"], "posr": ext["posr"], "ropec": ext["ropec"], "perm": ext["perm"],
               "o_u": X1[0:8], "o_q": X1[8:12], "o_k": X1[12:16],
               "o_v": X1[16:20].rearrange("c p (t d) -> c p t d", d=128),
               "o_sgs": sgs_d, "o_sgd": sgd_d, "o_yx": yx_d, "on_store": on_store, "tick": tick}
        finA = emit_phase_a(nc, P, ioA)
        assert sorted(g for _, g, _ in q_cc) == list(range(20))
        P.end_phase(finA)
        for gid in range(20):
            ev_cc[0] = allgather(X1[gid], g1dst(gid), [])
        ev_x1 = ev_cc[0]
        P.begin_phase("S%d" % l)

        def u_loader(u, du):
            ev = None
            uv = u.rearrange("p b (q t) -> p (b q) t", t=2048)
            ev0 = P.dma("sp", Lu[:, :], G1u[bass.ds(pid * 1024, 1024), :], du, [ev_x1])
            for half in range(2):
                src = Lu[half * 512:(half + 1) * 512, :].rearrange("(s p) t -> p s t", p=128)
                ev = P.dma("sp", uv[:, half * 4:(half + 1) * 4, :], src, du, [ev0])
            return ev

        ioS = {k: ext[k + sfx] for k in S_KEYS}
        ioS.update({"sg": ext["sg"], "bmask": ext["bmask"], "ramp": ext["ramp"], "ident": ext["ident"],
                    "u_loader": u_loader, "ys_dt": BF16,
                    "ys_dst": lambda g_, b, t0: X2[b * 4 + t0 // 2048][g_ * 16:(g_ + 1) * 16,
                                                                      (t0 % 2048):(t0 % 2048) + SBLK]})
        finS = emit_phase_s(nc, P, ioS)
        P.end_phase(finS)
        ev_x2 = None
        for j in range(8):
            ev_x2 = allgather(X2[j], G2s[j * 1024:(j + 1) * 1024, :], [])
        P.begin_phase("T%d" % l)
        tbase = (pid // 2) * 1024 + (pid % 2) * 512
        dsel = P.dsem()
        ev_sel = P.dma("sp", Lqkv[:, :, :],
                       G1qkv[bass.ds(tbase, 3 * 4096), :].rearrange("(k r) t -> k r t", r=4096)[:, 0:512, :],
                       dsel, [ev_x1])
        ioT = {"lq": ext["lq" + sfx], "gsub": ext["gsub" + sfx], "ident": ext["ident"], "yd_dt": BF16,
               "yd_dst": lambda c0: X2[8 + c0 // 2048][:, (c0 % 2048):(c0 % 2048) + 1024],
               "q_src": lambda i: Lqkv[0, i * 128:(i + 1) * 128, :],
               "k_src": lambda i: Lqkv[1, i * 128:(i + 1) * 128, :],
               "v_src": lambda i: Lqkv[2, i * 128:(i + 1) * 128, :].rearrange("p (t d) -> p t d", d=128),
               "ld_deps": [ev_sel]}
        finT = emit_phase_t(nc, P, ioT, lambda_init)
        P.end_phase(finT)
        for j in range(8, 12):
            ev_x2 = allgather(X2[j], G2d[(j - 8) * 1024:(j - 7) * 1024, :], [])
        P.begin_phase("C%d" % l)

        dsel2 = P.dsem()
        P.dma("sp", Lys[:, :], G2s[bass.ds(pid * 1024, 1024), :], dsel2, [ev_x2])
        cbase = (pid % 4) * 1024 + (pid // 4) * 128
        ev_sel2 = P.dma("sp", Lyd[:, :, :],
                        G2d[bass.ds(cbase, 1024), :].rearrange("(h r p) t -> h r p t", r=2, p=128)[:, 0, :, :],
                        dsel2)

        def ys_ld(dst, c0, c1, ld, deps):
            src = Lys.rearrange("(c p) t -> p c t", p=128)[:, :, c0:c1]
            return P.dma("sp", dst[:], src, ld, list(deps) + [ev_sel2])

        def yd_ld(dst, c0, c1, ld, deps):
            return P.dma("sp", dst[:], Lyd.rearrange("h p t -> p h t")[:, :, c0:c1], ld, list(deps) + [ev_sel2])

        ioC = {"sgs": sgs_d, "sgd": sgd_d, "yx": yx_d, "xT": x_src, "wglu": ext["wglu" + sfx],
               "bglu": ext["bglu" + sfx], "wout": ext["wout" + sfx], "gpost": ext["gpost" + sfx], "xnT": x_dst,
               "c_in_dt": BF16, "ys_ld": ys_ld, "yd_ld": yd_ld}
        finC = emit_phase_c(nc, P, ioC)
        P.end_phase(finC)
    return nc


def kernel_fused(**inputs):
    inputs = {k: np.asarray(v) for k, v in inputs.items()}
    ropec, perm = rope_consts()
    maps = []
    per_layer_common = []
    for l in range(DEPTH):
        tp = phase_t_params(inputs, l)
        per_layer_common.append({
            "wall": np.concatenate([wchunks(inputs["w_in"][l]), wchunks(inputs["w_mem_kv"][l])], axis=0),
            "gpre": np.ascontiguousarray(inputs["norm_pre"][l].reshape(KC, 128).T),
            "gmem": np.ascontiguousarray(inputs["norm_mem"][l].reshape(KC, 128).T),
            "lq": tp["lq"], "gsub": tp["gsub"],
            "wglu": wchunks(inputs["w_glu"][l]), "wout": wchunks(inputs["w_out"][l]),
            "bglu": np.ascontiguousarray(inputs["b_glu"][l].reshape(8, 128).T),
            "gpost": np.ascontiguousarray(inputs["norm_post"][l].reshape(KC, 128).T)})
    for c in range(NCORES):
        b, tq = c // 4, c % 4
        xs = inputs["x"][b, tq * TOK:(tq + 1) * TOK, :]
        m = {"xT": chunkT(np.ascontiguousarray(xs.T), KC),
             "memT": chunkT(np.ascontiguousarray(inputs["mem"][b].T), KC),
             "posr": np.ascontiguousarray(np.broadcast_to(inputs["positions"][b, tq * TOK:(tq + 1) * TOK][None, :],
                                                          (128, TOK))).astype(np.int32),
             "ropec": ropec, "perm": perm}
        for l in range(DEPTH):
            sp = phase_s_params(inputs, l, c)
            for k in ("ident", "ramp", "bmask", "sg"):
                m[k] = sp[k]
            for k in S_KEYS:
                m["%s_%d" % (k, l)] = sp[k]
            for k, v in per_layer_common[l].items():
                m["%s_%d" % (k, l)] = v
        maps.append(m)
    res = run_bass_kernel_spmd(build_fused(), maps, core_ids=list(range(NCORES))).results
    out = np.empty((BATCH, SEQ, D), np.float32)
    for c in range(NCORES):
        b, tq = c // 4, c % 4
        out[b, tq * TOK:(tq + 1) * TOK, :] = np.asarray(res[c]["out"]).transpose(1, 0, 2).reshape(D, TOK).T
    return out


def _run(nc, maps):
    res = run_bass_kernel_spmd(nc, maps, core_ids=list(range(NCORES)))
    return res.results


def kernel(**inputs):
    inputs = {k: np.asarray(v) for k, v in inputs.items()}
    depth = inputs["w_in"].shape[0]
    xT_cur = None
    for l in range(depth):
        lambda_init = 0.8 - 0.6 * math.exp(-0.3 * l)
        if l == 0:
            mapsA = phase_a_inputs(inputs, l, inputs["x"])
        else:
            mapsA = phase_a_inputs(inputs, l, None, xT_list=xT_cur)
        rA = _run(build_phase_a(), mapsA)
        xT_list = [m["xT"] for m in mapsA]
        mapsS = []
        for c in range(NCORES):
            m = phase_s_params(inputs, l, c)
            uT = np.empty((128, 2, SEQ), dtype=rA[0]["o_u"].dtype)
            for src in range(NCORES):
                b, tq = src // 4, src % 4
                uT[:, b, tq * TOK:(tq + 1) * TOK] = rA[src]["o_u"][c]
            m["uT"] = uT
            mapsS.append(m)
        rS = _run(build_phase_s(), mapsS)
        mapsT = []
        for c in range(NCORES):
            b, h = c // 4, c % 4
            m = phase_t_params(inputs, l)
            m["qT"] = np.ascontiguousarray(np.concatenate([rA[b * 4 + tq]["o_q"][h] for tq in range(4)], axis=1))
            m["kT"] = np.ascontiguousarray(np.concatenate([rA[b * 4 + tq]["o_k"][h] for tq in range(4)], axis=1))
            m["V"] = np.ascontiguousarray(np.concatenate(
                [np.asarray(rA[b * 4 + tq]["o_v"][h]).transpose(1, 0, 2).reshape(TOK, 128) for tq in range(4)], axis=0))
            mapsT.append(m)
        rT = _run(build_phase_t(lambda_init), mapsT)
        wglu = wchunks(inputs["w_glu"][l])
        wout = wchunks(inputs["w_out"][l])
        bglu = np.ascontiguousarray(inputs["b_glu"][l].reshape(8, 128).T)
        gpost = np.ascontiguousarray(inputs["norm_post"][l].reshape(KC, 128).T)
        mapsC = []
        for c in range(NCORES):
            b, tq = c // 4, c % 4
            sl = slice(tq * TOK, (tq + 1) * TOK)
            ysT = np.ascontiguousarray(np.stack([rS[g]["ys"][:, b, sl] for g in range(8)]))
            ydT = np.ascontiguousarray(np.stack([rT[b * 4 + h]["ydT"][:, sl] for h in range(4)]))
            mapsC.append({"ysT": ysT, "sgs": rA[c]["o_sgs"], "ydT": ydT, "sgd": rA[c]["o_sgd"], "yx": rA[c]["o_yx"],
                          "xT": xT_list[c], "wglu": wglu, "bglu": bglu, "wout": wout, "gpost": gpost})
        rC = _run(build_phase_c(), mapsC)
        xT_cur = [np.asarray(rC[c]["xnT"]) for c in range(NCORES)]
    out = np.empty((BATCH, SEQ, D), np.float32)
    for c in range(NCORES):
        b, tq = c // 4, c % 4
        out[b, tq * TOK:(tq + 1) * TOK, :] = xT_cur[c].transpose(1, 0, 2).reshape(D, TOK).T
    return out
```
